# Optimizing a Trainium2 kernel written in Bass

```python
import math, functools
import jax, jax.numpy as jnp
from jax import lax
import numpy as np

D_MODEL = 1024
BATCH = 2
SEQ = 16384
DEPTH = 4
DEC_BATCH = 16
DEC_SEQ = 64
PAST_LEN = 2048

CHUNK = 64
QBLOCK = 128
KBLOCK = 128
MIX_WIDTH = D_MODEL
IN_WIDTH = 4 * MIX_WIDTH
DIFF_HEADS = 4
DIFF_DH = 128
SB_HEADS = 4
SB_DH = 256
KV_WIDTH = MIX_WIDTH
ROPE_THETA = 10000.0
EPS = 1e-6
NEG_INF = -1e30
N_DIFF_LAYERS = (DEPTH + 1) // 2

kernel_name = "diff_stickbreak_streaming_step"


def _rmsnorm(x, g):
    xf = x.astype(jnp.float32)
    y = xf * lax.rsqrt(jnp.mean(xf * xf, axis=-1, keepdims=True) + EPS) * g.astype(jnp.float32)
    return y.astype(x.dtype)


def _rope(x, pos):
    half = x.shape[-1] // 2
    inv = ROPE_THETA ** (-jnp.arange(half, dtype=jnp.float32) / half)
    ang = pos.astype(jnp.float32)[:, None] * inv[None, :]
    cos = jnp.cos(ang)[None, :, None, :]
    sin = jnp.sin(ang)[None, :, None, :]
    xf = x.astype(jnp.float32)
    x1, x2 = xf[..., :half], xf[..., half:]
    return jnp.concatenate([x1 * cos - x2 * sin, x2 * cos + x1 * sin], axis=-1).astype(x.dtype)


def _rev_cumsum(x):
    S = x.shape[-1]
    nkb = -(-S // KBLOCK)
    pad = nkb * KBLOCK - S
    xp = jnp.pad(x, [(0, 0)] * (x.ndim - 1) + [(0, pad)])
    xb = xp.reshape(*x.shape[:-1], nkb, KBLOCK)
    ar = jnp.arange(KBLOCK)
    tri = (ar[:, None] >= ar[None, :]).astype(x.dtype)
    within = jnp.einsum('...nj,js->...ns', xb, tri, precision=lax.Precision.HIGHEST)
    tot = jnp.sum(xb, axis=-1)
    an = jnp.arange(nkb)
    btri = (an[:, None] > an[None, :]).astype(x.dtype)
    after = jnp.einsum('...m,mn->...n', tot, btri, precision=lax.Precision.HIGHEST)
    out = (within + after[..., None]).reshape(*x.shape[:-1], nkb * KBLOCK)
    return out[..., :S]


def _attend(block_fn, q, k_all, v_all, P):
    B, T = q.shape[0], q.shape[1]
    nb = T // QBLOCK if (T % QBLOCK == 0 and T > QBLOCK) else 1
    blk = T // nb
    outs = []
    for b in range(nb):
        end = P + (b + 1) * blk
        q_pos = P + b * blk + jnp.arange(blk, dtype=jnp.int32)
        k_pos = jnp.arange(end, dtype=jnp.int32)
        outs.append(block_fn(q[:, b * blk:(b + 1) * blk], q_pos,
                             k_all[:, :end], v_all[:, :end], k_pos))
    return outs[0] if nb == 1 else jnp.concatenate(outs, axis=1)


def _diff_block(q_blk, qpos_blk, k_all, v_all, kpos, lam):
    s = jnp.einsum('bqhmd,bkhmd->bhmqk', q_blk, k_all,
                   preferred_element_type=jnp.float32) * (DIFF_DH ** -0.5)
    visible = (kpos // CHUNK)[None, :] <= (qpos_blk // CHUNK)[:, None]
    s = jnp.where(visible, s, NEG_INF)
    p = jax.nn.softmax(s, axis=-1)
    w = p[:, :, 0] - lam * p[:, :, 1]
    return jnp.einsum('bhqk,bkhe->bqhe', w.astype(v_all.dtype), v_all)


def _sb_block(q_blk, qpos_blk, k_all, v_all, kpos):
    z = jnp.einsum('bqhd,bkhd->bhqk', q_blk, k_all,
                   preferred_element_type=jnp.float32) * (SB_DH ** -0.5)
    earlier = kpos[None, :] < qpos_blk[:, None]
    log_keep = jnp.where(earlier, jax.nn.log_sigmoid(-z), 0.0)
    a = jnp.where(earlier, jnp.exp(z + _rev_cumsum(log_keep)), 0.0)
    return jnp.einsum('bhqk,bkhd->bqhd', a.astype(v_all.dtype), v_all)


def _diff_mixer(h, past_k, past_v, w_in_l, w_out_l, lam_vecs, subln_w, lam_init):
    B, T, _ = h.shape
    P = 0 if past_k is None else past_k.shape[1]
    q_pos = P + jnp.arange(T, dtype=jnp.int32)
    q, k, v, g = jnp.split(h @ w_in_l, 4, axis=-1)
    q = _rope(q.reshape(B, T, 2 * DIFF_HEADS, DIFF_DH), q_pos).reshape(B, T, DIFF_HEADS, 2, DIFF_DH)
    k_rows = _rope(k.reshape(B, T, 2 * DIFF_HEADS, DIFF_DH), q_pos).reshape(B, T, KV_WIDTH)
    v_rows = v
    if past_k is None:
        k_all, v_all = k_rows, v_rows
    else:
        k_all = jnp.concatenate([past_k, k_rows], axis=1)
        v_all = jnp.concatenate([past_v, v_rows], axis=1)
    lf = lam_vecs.astype(jnp.float32)
    lam = jnp.exp(jnp.sum(lf[0] * lf[1])) - jnp.exp(jnp.sum(lf[2] * lf[3])) + lam_init
    block = functools.partial(_diff_block, lam=lam)
    o = _attend(block, q,
                k_all.reshape(B, P + T, DIFF_HEADS, 2, DIFF_DH),
                v_all.reshape(B, P + T, DIFF_HEADS, 2 * DIFF_DH), P)
    o = _rmsnorm(o, subln_w) * (1.0 - lam_init)
    o = o.reshape(B, T, MIX_WIDTH) * jax.nn.silu(g)
    return o @ w_out_l, k_rows, v_rows


def _sb_mixer(h, past_k, past_v, w_in_l, w_out_l):
    B, T, _ = h.shape
    P = 0 if past_k is None else past_k.shape[1]
    q, k_rows, v_rows, g = jnp.split(h @ w_in_l, 4, axis=-1)
    if past_k is None:
        k_all, v_all = k_rows, v_rows
    else:
        k_all = jnp.concatenate([past_k, k_rows], axis=1)
        v_all = jnp.concatenate([past_v, v_rows], axis=1)
    o = _attend(_sb_block, q.reshape(B, T, SB_HEADS, SB_DH),
                k_all.reshape(B, P + T, SB_HEADS, SB_DH),
                v_all.reshape(B, P + T, SB_HEADS, SB_DH), P)
    o = o.reshape(B, T, MIX_WIDTH) * jax.nn.silu(g)
    return o @ w_out_l, k_rows, v_rows


def setup_inputs(seed: int = 0) -> dict:
    key = jax.random.key(seed)
    ks = jax.random.split(key, 10)
    f32 = jnp.float32
    x_prompt = jax.random.normal(ks[0], (BATCH, SEQ, D_MODEL), f32)
    x_sample = jax.random.normal(ks[1], (DEC_BATCH, DEC_SEQ, D_MODEL), f32)
    cache_k = jax.random.normal(ks[2], (DEPTH, DEC_BATCH, PAST_LEN, KV_WIDTH), f32)
    cache_v = jax.random.normal(ks[3], (DEPTH, DEC_BATCH, PAST_LEN, KV_WIDTH), f32)
    norm_w = 1.0 + 0.01 * jax.random.normal(ks[4], (DEPTH, D_MODEL), f32)
    w_in = jax.random.normal(ks[5], (DEPTH, D_MODEL, IN_WIDTH), f32) * (D_MODEL ** -0.5)
    w_out = jax.random.normal(ks[6], (DEPTH, MIX_WIDTH, D_MODEL), f32) * (MIX_WIDTH ** -0.5)
    diff_lambda = 0.1 * jax.random.normal(ks[7], (N_DIFF_LAYERS, 4, DIFF_DH), f32)
    diff_subln_w = 1.0 + 0.01 * jax.random.normal(ks[8], (N_DIFF_LAYERS, 2 * DIFF_DH), f32)
    final_norm_w = 1.0 + 0.01 * jax.random.normal(ks[9], (D_MODEL,), f32)
    return {"x_prompt": x_prompt, "x_sample": x_sample, "cache_k": cache_k, "cache_v": cache_v,
            "norm_w": norm_w, "w_in": w_in, "w_out": w_out, "diff_lambda": diff_lambda,
            "diff_subln_w": diff_subln_w, "final_norm_w": final_norm_w}


def reference(x_prompt, x_sample, cache_k, cache_v, norm_w, w_in, w_out, diff_lambda,
              diff_subln_w, final_norm_w):
    yp, ys = x_prompt, x_sample
    kp_list, vp_list, ks_list, vs_list = [], [], [], []
    for i in range(DEPTH):
        hp = _rmsnorm(yp, norm_w[i])
        hs = _rmsnorm(ys, norm_w[i])
        if i % 2 == 0:
            j = i // 2
            lam_init = 0.8 - 0.6 * math.exp(-0.3 * i)
            mix = functools.partial(_diff_mixer, w_in_l=w_in[i], w_out_l=w_out[i],
                                    lam_vecs=diff_lambda[j], subln_w=diff_subln_w[j],
                                    lam_init=lam_init)
        else:
            mix = functools.partial(_sb_mixer, w_in_l=w_in[i], w_out_l=w_out[i])
        op, kp, vp = mix(hp, None, None)
        os_, ks_, vs_ = mix(hs, cache_k[i], cache_v[i])
        yp = yp + op
        ys = ys + os_
        kp_list.append(kp)
        vp_list.append(vp)
        ks_list.append(ks_)
        vs_list.append(vs_)
    y_prompt = _rmsnorm(yp, final_norm_w)
    y_sample = _rmsnorm(ys, final_norm_w)
    new_k_prompt = jnp.stack(kp_list, axis=0)
    new_v_prompt = jnp.stack(vp_list, axis=0)
    new_k_sample = jnp.stack(ks_list, axis=0)
    new_v_sample = jnp.stack(vs_list, axis=0)
    return (y_prompt, y_sample, new_k_prompt, new_v_prompt, new_k_sample, new_v_sample)
```

```python
import math
from contextlib import ExitStack

import numpy as np
import concourse.bass as bass
import concourse.mybir as mybir
from concourse.bass_utils import run_bass_kernel_spmd

F32 = mybir.dt.float32
BF16 = mybir.dt.bfloat16
AF = mybir.ActivationFunctionType
ALU = mybir.AluOpType
AX = mybir.AxisListType

D = 1024
HD = 256
NS = 8
DS = 64
QW = 256
CW = 512
EPS = 1e-6
DIFF_SCALE = 128 ** -0.5
SB_SCALE = 256 ** -0.5


class Ev:
    __slots__ = ("sem", "val", "eng")

    def __init__(self, sem, val, eng):
        self.sem, self.val, self.eng = sem, val, eng


class Buf:
    def __init__(self, name):
        self.name = name
        self.w = None
        self.r = {}


class Prog:
    ENG = ("pe", "act", "dve", "pool", "sp")

    def __init__(self, nc, stack):
        self.nc = nc
        self.stack = stack
        self.q = {e: [] for e in self.ENG}
        self.esem = {e: stack.enter_context(nc.semaphore("es_" + e)) for e in ("pe", "act", "dve", "pool")}
        self.ecnt = {e: 0 for e in self.esem}
        self.dsem = {}
        self.dcnt = {}
        self.ccsem = stack.enter_context(nc.semaphore("cc_sem"))
        self.cccnt = 0
        self.seen = {e: {} for e in self.ENG}

    def _waits(self, eng, reads, writes):
        evs = []
        for b in reads:
            if b.w is not None:
                evs.append(b.w)
        for b in writes:
            if b.w is not None:
                evs.append(b.w)
            evs.extend(b.r.values())
        need = {}
        seen = self.seen[eng]
        for ev in evs:
            if eng == "pe" and ev.eng == "pe":
                continue
            k = id(ev.sem)
            if seen.get(k, 0) >= ev.val:
                continue
            if k not in need or need[k][1] < ev.val:
                need[k] = (ev.sem, ev.val)
        for k, (s, v) in need.items():
            seen[k] = v
        return list(need.values())

    def _commit(self, ev, reads, writes):
        for b in reads:
            k = id(ev.sem)
            b.r[k] = ev
        for b in writes:
            b.w = ev
            b.r = {}

    def op(self, eng, fn, reads=(), writes=()):
        waits = self._waits(eng, reads, writes)
        self.ecnt[eng] += 1
        ev = Ev(self.esem[eng], self.ecnt[eng], eng)

        def run(e, waits=waits, fn=fn, ev=ev):
            for s, v in waits:
                e.wait_ge(s, v)
            fn(e).then_inc(ev.sem, 1)

        self.q[eng].append(run)
        self._commit(ev, reads, writes)
        return ev

    def dma(self, eng, key, out, in_, reads=(), writes=()):
        if key not in self.dsem:
            self.dsem[key] = self.stack.enter_context(self.nc.semaphore("ds%d" % len(self.dsem)))
            self.dcnt[key] = 0
        waits = self._waits(eng, reads, writes)
        self.dcnt[key] += 16
        ev = Ev(self.dsem[key], self.dcnt[key], "dma")

        def run(e, waits=waits, ev=ev, out=out, in_=in_):
            for s, v in waits:
                e.wait_ge(s, v)
            e.dma_start(out=out, in_=in_).then_inc(ev.sem, 16)

        self.q[eng].append(run)
        self._commit(ev, reads, writes)
        return ev

    def allgather(self, src, dst, groups, reads=(), writes=()):
        waits = self._waits("pool", reads, writes)
        self.cccnt += 1
        ev = Ev(self.ccsem, self.cccnt, "cc")

        def run(e, waits=waits, ev=ev):
            for s, v in waits:
                e.wait_ge(s, v)
            e.collective_compute("AllGather", ALU.bypass, replica_groups=groups,
                                 ins=[src], outs=[dst]).then_inc(ev.sem)

        self.q["pool"].append(run)
        self._commit(ev, reads, writes)
        return ev

    def barrier(self):
        evs = [(self.esem[e], self.ecnt[e]) for e in self.esem if self.ecnt[e] > 0]
        evs += [(self.dsem[k], self.dcnt[k]) for k in self.dsem if self.dcnt[k] > 0]
        if self.cccnt > 0:
            evs.append((self.ccsem, self.cccnt))
        for eng in self.ENG:
            need = []
            seen = self.seen[eng]
            for s, v in evs:
                if seen.get(id(s), 0) >= v:
                    continue
                seen[id(s)] = v
                need.append((s, v))

            def run(e, need=need):
                for s, v in need:
                    e.wait_ge(s, v)

            self.q[eng].append(run)


def build_program(SEQ, PAST, DEPTH):
    NPT = SEQ // 128
    TT = SEQ + NS * DS
    NQT = SEQ // QW
    NCH = TT // CW
    NPB = PAST // 128
    NTILE = NPT + NS
    assert SEQ % CW == 0 and PAST % 512 == 0

    nc = bass.Bass("TRN2", target_bir_lowering=False)
    dt = nc.dram_tensor
    xin = dt("xin", [TT, D], F32, kind="ExternalInput").ap()
    w_in = dt("w_in_h", [DEPTH, D, D], F32, kind="ExternalInput").ap()
    w_out = dt("w_out", [DEPTH, D, D], F32, kind="ExternalInput").ap()
    normw = dt("normw_b", [DEPTH + 1, 128, D], F32, kind="ExternalInput").ap()
    lam_in = dt("lam_b", [2, 128, 512], F32, kind="ExternalInput").ap()
    subw_in = dt("subw_b", [2, 128, HD], F32, kind="ExternalInput").ap()
    ck = dt("ck", [DEPTH, NS, PAST, HD], F32, kind="ExternalInput").ap()
    cv = dt("cv", [DEPTH, NS, PAST, HD], F32, kind="ExternalInput").ap()
    rope = dt("rope", [TT, 192], F32, kind="ExternalInput").ap()
    cst = dt("cst", [128, 1344], F32, kind="ExternalInput").ap()
    y_out = dt("y", [TT, D], F32, kind="ExternalOutput").ap()
    kout = dt("kout", [DEPTH, TT, HD], F32, kind="ExternalOutput").ap()
    vout = dt("vout", [DEPTH, TT, HD], F32, kind="ExternalOutput").ap()
    xs = dt("xs", [TT, D], F32, kind="Internal").ap()
    qT = dt("qT", [2, 128, TT], BF16, kind="Internal").ap()
    sg = dt("sg", [TT, HD], F32, kind="Internal").ap()
    ag_src = dt("ag_src", [NCH, 256, CW], BF16, kind="Internal").ap()
    ag_dst = dt("ag_dst", [NCH, 1024, CW], BF16, kind="Internal").ap()
    GROUPS = [[0, 1, 2, 3], [4, 5, 6, 7]]

    stack = ExitStack()
    with stack:
        sb = lambda name, shape, dtp: stack.enter_context(nc.sbuf_tensor(name, shape, dtp))
        KT = sb("KT", [128, 2, SEQ], BF16)
        VT = sb("VT", [128, NPT, 257], BF16)
        KTN = sb("KTN", [128, 2, NS * DS], BF16)
        VN = sb("VN", [128, NS, 257], BF16)
        CB = sb("CB", [128, 1344], BF16)
        NEGONES = sb("NEGONES", [128, 128], BF16)
        SW = sb("SW", [128, 2, HD], F32)
        NEGLAM = sb("NEGLAM", [128, 2], F32)
        SMALL = sb("SMALL", [128, 16], F32)
        ZERO1 = sb("ZERO1", [128, 1], F32)
        EPSB = sb("EPSB", [128, 1], F32)
        A16 = sb("A16", [128, 21760], BF16)
        A32 = sb("A32", [128, 5376], F32)
        PS = [stack.enter_context(nc.psum_tensor("ps%d" % i, [128, 512], F32)) for i in range(8)]
        block = stack.enter_context(nc.Block())

        MASKS = CB[:, 0:1024].rearrange("p (r q) -> p r q", r=4)
        MSAMP = CB[:, 1024:1088]
        IDENT = CB[:, 1088:1216]
        TRI = CB[:, 1216:1344]

        class Carver:
            def __init__(self, t):
                self.t, self.o = t, 0

            def take(self, n):
                v = self.t[:, self.o:self.o + n]
                self.o += n
                assert self.o <= self.t.shape[1], (self.o, self.t.shape)
                return v

        c16 = Carver(A16)
        WI = c16.take(8192).rearrange("p (c n) -> p c n", c=8)
        WO = c16.take(8192).rearrange("p (c n) -> p c n", c=8)
        Gs = [c16.take(1024).rearrange("p (c t) -> p c t", c=8) for _ in range(2)]
        Hb = c16.take(1024)
        HT = c16.take(1024).rearrange("p (c t) -> p c t", c=8)
        QKB = c16.take(512)
        QTst = c16.take(256).rearrange("p (c t) -> p c t", c=2)
        c32 = Carver(A32)
        Xs = [c32.take(1024) for _ in range(2)]
        NW = c32.take(1024)
        QK = c32.take(512)
        TMPr = c32.take(512)
        VF = c32.take(256)
        SGs = c32.take(256)
        ROPE = [c32.take(192) for _ in range(2)]
        a16 = Carver(A16)
        QTa = [a16.take(512).rearrange("p (c t) -> p c t", c=2) for _ in range(2)]
        Pt = [a16.take(512).rearrange("p (m t) -> p m t", m=2) for _ in range(3)]
        LKt = [a16.take(256) for _ in range(2)]
        OGB = a16.take(256)
        OGT = [a16.take(512).rearrange("p (c t) -> p c t", c=2) for _ in range(2)]
        KPB = a16.take(1024).rearrange("p (j e) -> p j e", j=4)
        KTS = a16.take(2 * PAST).rearrange("p (c t) -> p c t", c=2)
        VS = a16.take(NPB * 257).rearrange("p (j e) -> p j e", j=NPB)
        a32 = Carver(A32)
        SGt = [a32.take(512).rearrange("p (u e) -> p u e", u=2) for _ in range(2)]
        Et = [a32.take(256) for _ in range(2)]
        TMt = [a32.take(256) for _ in range(2)]
        Ot = a32.take(256)
        O2t = a32.take(256)

        P = Prog(nc, stack)

        B = {}

        def buf(name):
            if name not in B:
                B[name] = Buf(name)
            return B[name]

        psb = [buf("ps%d" % i) for i in range(8)]

        P.dma("pool", "cst", CB[:], cst, writes=[buf("CB")])
        P.op("pool", lambda e: e.memset(NEGONES[:], -1.0), writes=[buf("NEGONES")])
        P.op("pool", lambda e: e.memset(VT[:, :, 256:257], 1.0), writes=[buf("VTones")])
        P.op("pool", lambda e: e.memset(VN[:, :, 256:257], 1.0), writes=[buf("VNones")])
        P.op("pool", lambda e: e.memset(ZERO1[:], 0.0), writes=[buf("ZERO1")])
        P.op("pool", lambda e: e.memset(EPSB[:], EPS), writes=[buf("EPSB")])
        LAMT = A32[:, 0:512].rearrange("p (a d) -> p a d", a=4)
        LJ = A32[:, 512:640]
        for j in range(2):
            if 2 * j >= DEPTH:
                break
            lam_init = 0.8 - 0.6 * math.exp(-0.3 * (2 * j))
            P.dma("sp", "lam", LAMT, lam_in[j].rearrange("p (a d) -> p a d", a=4), writes=[buf("LAMT")])
            P.dma("sp", "subw", SW[:, j, :], subw_in[j], writes=[buf("SWraw")])
            for t in range(2):
                P.op("dve", lambda e, t=t: e.tensor_tensor(out=LJ, in0=LAMT[:, 2 * t, :], in1=LAMT[:, 2 * t + 1, :], op=ALU.mult),
                     reads=[buf("LAMT")], writes=[buf("LJ")])
                P.op("dve", lambda e, t=t: e.tensor_reduce(out=SMALL[:, t:t + 1], in_=LJ, axis=AX.X, op=ALU.add),
                     reads=[buf("LJ")], writes=[buf("SMALL")])
            P.op("act", lambda e: e.activation(out=SMALL[:, 2:4], in_=SMALL[:, 0:2], func=AF.Exp),
                 reads=[buf("SMALL")], writes=[buf("SMALL")])
            P.op("dve", lambda e: e.tensor_tensor(out=SMALL[:, 4:5], in0=SMALL[:, 3:4], in1=SMALL[:, 2:3], op=ALU.subtract),
                 reads=[buf("SMALL")], writes=[buf("SMALL")])
            P.op("dve", lambda e, j=j, li=lam_init: e.tensor_scalar(out=NEGLAM[:, j:j + 1], in0=SMALL[:, 4:5], scalar1=-li, scalar2=None, op0=ALU.add),
                 reads=[buf("SMALL")], writes=[buf("NEGLAM")])
            P.op("dve", lambda e, j=j, li=lam_init: e.tensor_scalar(out=SW[:, j, :], in0=SW[:, j, :], scalar1=1.0 - li, scalar2=None, op0=ALU.mult),
                 reads=[buf("SWraw")], writes=[buf("SWraw")])
        P.barrier()

        def tile_info(ti):
            if ti < NPT:
                return ti * 128, 128
            return SEQ + (ti - NPT) * DS, DS

        def f_phase(l):
            last = (l == DEPTH)
            is_diff = (l % 2 == 0)
            if not last:
                P.dma("pool", "WI", WI, w_in[l].rearrange("(c p) n -> p c n", p=128), writes=[buf("WI")])
            if l > 0:
                P.dma("pool", "WO", WO, w_out[l - 1].rearrange("(c p) n -> p c n", p=128), writes=[buf("WO")])
            P.dma("sp", "NW", NW, normw[l], writes=[buf("NW")])
            Y = [PS[0], PS[1]]
            HTp = PS[2][:].bitcast(BF16).rearrange("p (c t) -> p c t", c=8)
            PJ = [PS[3], PS[4]]
            QKTp = PS[5][:].bitcast(BF16)[:, 0:512].rearrange("p (c t) -> p c t", c=4)
            for ti in range(NTILE):
                t0, n = tile_info(ti)
                sl = ti % 2
                X = Xs[sl]
                Xb = buf("X%d" % sl)
                P.dma("sp", ("X", sl), X[0:n, :], (xin if l == 0 else xs)[t0:t0 + n, :], writes=[Xb])
                if l > 0:
                    G = Gs[sl]
                    Gb = buf("G%d" % sl)
                    ch, off = t0 // CW, t0 % CW
                    P.dma("sp", ("G", sl), G[:, :, 0:n],
                          ag_dst[ch].rearrange("(c p) t -> p c t", p=128)[:, :, off:off + n], writes=[Gb])

                    def mm_out(e, G=G, n=n):
                        ins = None
                        for nn in range(2):
                            for c in range(8):
                                ins = e.matmul(Y[nn][0:n, :], lhsT=G[:, c, 0:n], rhs=WO[:, c, nn * 512:(nn + 1) * 512],
                                               start=(c == 0), stop=(c == 7))
                        return ins
                    P.op("pe", mm_out, reads=[Gb, buf("WO")], writes=[psb[0], psb[1]])
                    for nn in range(2):
                        P.op("dve", lambda e, X=X, n=n, nn=nn: e.tensor_tensor(
                            out=X[0:n, nn * 512:(nn + 1) * 512], in0=X[0:n, nn * 512:(nn + 1) * 512],
                            in1=Y[nn][0:n, :], op=ALU.add), reads=[psb[nn], Xb], writes=[Xb])
                if not last:
                    P.dma("pool", ("Xst", sl), xs[t0:t0 + n, :], X[0:n, :], reads=[Xb])
                SSb = buf("SS")
                P.op("act", lambda e, X=X, n=n: e.activation(out=Hb[0:n, :], in_=X[0:n, :], func=AF.Square,
                                                              accum_out=SMALL[0:n, 8:9]),
                     reads=[Xb], writes=[buf("H"), SSb])
                P.op("act", lambda e, n=n: e.activation(out=SMALL[0:n, 9:10], in_=SMALL[0:n, 8:9], func=AF.Ln,
                                                        scale=1.0 / D, bias=EPSB[0:n, 0:1]),
                     reads=[SSb], writes=[buf("RS0")])
                P.op("act", lambda e, n=n: e.activation(out=SMALL[0:n, 10:11], in_=SMALL[0:n, 9:10], func=AF.Exp, scale=-0.5),
                     reads=[buf("RS0")], writes=[buf("RS")])
                if last:
                    P.op("dve", lambda e, X=X, n=n: e.scalar_tensor_tensor(
                        out=X[0:n, :], in0=X[0:n, :], scalar=SMALL[0:n, 10:11], in1=NW[0:n, :], op0=ALU.mult, op1=ALU.mult),
                        reads=[Xb, buf("RS"), buf("NW")], writes=[Xb])
                    P.dma("pool", ("Yst", sl), y_out[t0:t0 + n, :], X[0:n, :], reads=[Xb])
                    continue
                P.op("dve", lambda e, X=X, n=n: e.scalar_tensor_tensor(
                    out=Hb[0:n, :], in0=X[0:n, :], scalar=SMALL[0:n, 10:11], in1=NW[0:n, :], op0=ALU.mult, op1=ALU.mult),
                    reads=[Xb, buf("RS"), buf("NW")], writes=[buf("H")])

                def tr_h(e, n=n):
                    ins = None
                    for c in range(8):
                        ins = e.transpose(out=HTp[:, c, 0:n], in_=Hb[0:n, c * 128:(c + 1) * 128], identity=IDENT[0:n, 0:n])
                    return ins
                P.op("pe", tr_h, reads=[buf("H"), buf("CB")], writes=[psb[2]])
                P.op("act", lambda e, n=n: e.activation(out=HT[:, :, 0:n], in_=HTp[:, :, 0:n], func=AF.Copy),
                     reads=[psb[2]], writes=[buf("HT")])

                def mm_in(e, n=n):
                    ins = None
                    for nn in range(2):
                        for c in range(8):
                            ins = e.matmul(PJ[nn][0:n, :], lhsT=HT[:, c, 0:n], rhs=WI[:, c, nn * 512:(nn + 1) * 512],
                                           start=(c == 0), stop=(c == 7))
                    return ins
                P.op("pe", mm_in, reads=[buf("HT"), buf("WI")], writes=[psb[3], psb[4]])
                QKb = buf("QK")
                if is_diff:
                    R = ROPE[sl]
                    Rb = buf("ROPE%d" % sl)
                    P.dma("sp", ("ROPE", sl), R[0:n, :], rope[t0:t0 + n, :], writes=[Rb])
                    pj8 = PJ[0][0:n, :].rearrange("p (m d) -> p m d", m=8)
                    pj4 = PJ[0][0:n, :].rearrange("p (m h d) -> p m h d", m=4, h=2)
                    tm4 = TMPr[0:n, :].rearrange("p (m h d) -> p m h d", m=4, h=2)
                    P.op("dve", lambda e, n=n, R=R, pj8=pj8: e.tensor_tensor(
                        out=QK[0:n, :].rearrange("p (m d) -> p m d", m=8), in0=pj8,
                        in1=R[0:n, 0:64].unsqueeze(1).broadcast_to([n, 8, 64]), op=ALU.mult),
                        reads=[psb[3], Rb], writes=[QKb])
                    P.op("dve", lambda e, n=n, R=R, pj4=pj4, tm4=tm4: e.tensor_tensor(
                        out=tm4[:, :, 0, :], in0=pj4[:, :, 1, :],
                        in1=R[0:n, 64:128].unsqueeze(1).broadcast_to([n, 4, 64]), op=ALU.mult),
                        reads=[psb[3], Rb], writes=[buf("TMPa")])
                    P.op("dve", lambda e, n=n, R=R, pj4=pj4, tm4=tm4: e.tensor_tensor(
                        out=tm4[:, :, 1, :], in0=pj4[:, :, 0, :],
                        in1=R[0:n, 128:192].unsqueeze(1).broadcast_to([n, 4, 64]), op=ALU.mult),
                        reads=[psb[3], Rb], writes=[buf("TMPb")])
                    P.op("dve", lambda e, n=n: e.tensor_tensor(out=QK[0:n, :], in0=QK[0:n, :], in1=TMPr[0:n, :], op=ALU.add),
                         reads=[QKb, buf("TMPa"), buf("TMPb")], writes=[QKb])
                else:
                    P.op("dve", lambda e, n=n: e.tensor_copy(out=QK[0:n, :], in_=PJ[0][0:n, :]),
                         reads=[psb[3]], writes=[QKb])
                P.dma("pool", "kst", kout[l, t0:t0 + n, :], QK[0:n, 256:512], reads=[QKb])
                qs = 1.0 if is_diff else SB_SCALE
                P.op("act", lambda e, n=n, qs=qs: e.activation(out=QKB[0:n, 0:256], in_=QK[0:n, 0:256], func=AF.Copy, scale=qs),
                     reads=[QKb], writes=[buf("QKBq")])
                P.op("act", lambda e, n=n: e.activation(out=QKB[0:n, 256:512], in_=QK[0:n, 256:512], func=AF.Copy),
                     reads=[QKb], writes=[buf("QKBk")])

                def tr_qk(e, n=n):
                    ins = None
                    for c in range(4):
                        ins = e.transpose(out=QKTp[:, c, 0:n], in_=QKB[0:n, c * 128:(c + 1) * 128], identity=IDENT[0:n, 0:n])
                    return ins
                P.op("pe", tr_qk, reads=[buf("QKBq"), buf("QKBk")], writes=[psb[5]])
                P.op("act", lambda e, n=n: e.activation(out=QTst[:, :, 0:n], in_=QKTp[:, 0:2, 0:n], func=AF.Copy),
                     reads=[psb[5]], writes=[buf("QTst")])
                if ti < NPT:
                    kdst = KT[:, :, t0:t0 + n]
                    vdst = VT[0:n, ti, 0:256]
                else:
                    s = ti - NPT
                    kdst = KTN[:, :, s * DS:(s + 1) * DS]
                    vdst = VN[0:n, s, 0:256]
                P.op("act", lambda e, n=n, kdst=kdst: e.activation(out=kdst, in_=QKTp[:, 2:4, 0:n], func=AF.Copy),
                     reads=[psb[5]], writes=[buf("KTw")])
                P.dma("pool", "qst", qT[:, :, t0:t0 + n].rearrange("c p t -> p c t"), QTst[:, :, 0:n], reads=[buf("QTst")])
                P.op("dve", lambda e, n=n: e.tensor_copy(out=VF[0:n, :], in_=PJ[1][0:n, 0:256]),
                     reads=[psb[4]], writes=[buf("VF")])
                P.op("dve", lambda e, n=n, vdst=vdst: e.tensor_copy(out=vdst, in_=PJ[1][0:n, 0:256]),
                     reads=[psb[4]], writes=[buf("VTw")])
                P.dma("pool", "vst", vout[l, t0:t0 + n, :], VF[0:n, :], reads=[buf("VF")])
                P.op("act", lambda e, n=n: e.activation(out=SGs[0:n, :], in_=PJ[1][0:n, 256:512], func=AF.Silu),
                     reads=[psb[4]], writes=[buf("SGs")])
                P.dma("pool", "gst", sg[t0:t0 + n, :], SGs[0:n, :], reads=[buf("SGs")])
            P.barrier()

        state = {"step": 0, "qt": 0}

        def attend(l, qd, kds):
            is_diff = (l % 2 == 0)
            qw, nsub, sw_ = qd["qw"], qd["nsub"], qd["subw"]
            QT, QTb, SGv, SGb = qd["QT"], qd["QTb"], qd["SG"], qd["SGb"]
            nk_tot = len(kds)
            if is_diff:
                ACC = [[PS[2 + 2 * m + u] for u in range(2)] for m in range(2)]
                accb = [psb[2], psb[3], psb[4], psb[5]]
            else:
                ACC = [PS[2][:, 0:256], PS[2][:, 256:512]]
                accb = [psb[2]]
                CACC = PS[6]
            for i, kd in enumerate(kds):
                st = state["step"]
                state["step"] += 1
                nk = kd["nk"]
                xr = list(kd.get("reads", ()))
                Sb = psb[st % 2]
                Pv = Pt[st % 3]
                Pb = buf("P%d" % (st % 3))
                if is_diff:
                    S = PS[st % 2][:].rearrange("p (m t) -> p m t", m=2)

                    def mm_qk(e, S=S, kd=kd, nk=nk):
                        ins = None
                        for m in range(2):
                            ins = e.matmul(S[0:nk, m, 0:qw], lhsT=kd["K"][m], rhs=QT[:, m, 0:qw], start=True, stop=True)
                        return ins
                    P.op("pe", mm_qk, reads=[QTb] + xr, writes=[Sb])
                    P.op("act", lambda e, S=S, Pv=Pv, nk=nk: e.activation(out=Pv[0:nk, :, 0:qw], in_=S[0:nk, :, 0:qw],
                                                                         func=AF.Exp, scale=DIFF_SCALE),
                         reads=[Sb], writes=[Pb])
                    if kd.get("mask") is not None:
                        mk = kd["mask"]
                        P.op("dve", lambda e, Pv=Pv, nk=nk, mk=mk: e.tensor_tensor(
                            out=Pv[0:nk, :, 0:qw], in0=Pv[0:nk, :, 0:qw],
                            in1=mk.unsqueeze(1).broadcast_to([nk, 2, qw]), op=ALU.mult),
                            reads=[Pb, buf("CB")], writes=[Pb])

                    def mm_pv(e, Pv=Pv, kd=kd, nk=nk, i=i):
                        ins = None
                        for m in range(2):
                            for u in range(nsub):
                                ins = e.matmul(ACC[m][u][0:sw_, 0:257], lhsT=Pv[0:nk, m, u * sw_:(u + 1) * sw_],
                                               rhs=kd["V"][0:nk, 0:257], start=(i == 0), stop=(i == nk_tot - 1))
                        return ins
                    P.op("pe", mm_pv, reads=[Pb] + xr, writes=accb)
                else:
                    Z = PS[st % 2]
                    E = Et[st % 2]
                    Eb = buf("E%d" % (st % 2))
                    LK = LKt[st % 2]
                    LKb = buf("LK%d" % (st % 2))
                    TM = TMt[st % 2]
                    TMb = buf("TM%d" % (st % 2))
                    Pv2 = Pv[:, 0, :]

                    def mm_z(e, Z=Z, kd=kd, nk=nk):
                        ins = None
                        for c in range(2):
                            ins = e.matmul(Z[0:nk, 0:qw], lhsT=kd["K"][c], rhs=QT[:, c, 0:qw], start=(c == 0), stop=False)
                        return ins
                    P.op("pe", mm_z, reads=[QTb] + xr, writes=[Sb])
                    P.op("act", lambda e, Z=Z, E=E, nk=nk: e.activation(out=E[0:nk, 0:qw], in_=Z[0:nk, 0:qw], func=AF.Exp),
                         reads=[Sb], writes=[Eb])
                    P.op("act", lambda e, E=E, LK=LK, nk=nk: e.activation(out=LK[0:nk, 0:qw], in_=E[0:nk, 0:qw], func=AF.Ln, bias=1.0),
                         reads=[Eb], writes=[LKb])
                    if kd.get("mask") is not None:
                        mk = kd["mask"]
                        P.op("dve", lambda e, LK=LK, nk=nk, mk=mk: e.tensor_tensor(
                            out=LK[0:nk, 0:qw], in0=LK[0:nk, 0:qw], in1=mk, op=ALU.mult),
                            reads=[LKb, buf("CB")], writes=[LKb])

                    def mm_tri(e, Z=Z, LK=LK, nk=nk, i=i):
                        e.matmul(Z[0:nk, 0:qw], lhsT=TRI[0:nk, 0:nk], rhs=LK[0:nk, 0:qw], start=False, stop=True)
                        return e.matmul(CACC[:, 0:qw], lhsT=NEGONES[0:nk, :], rhs=LK[0:nk, 0:qw],
                                        start=(i == 0), stop=(i == nk_tot - 1))
                    P.op("pe", mm_tri, reads=[LKb, buf("CB"), buf("NEGONES")], writes=[Sb, psb[6]])
                    P.op("dve", lambda e, TM=TM, nk=nk: e.tensor_copy(out=TM[0:nk, 0:qw], in_=CACC[0:nk, 0:qw]),
                         reads=[psb[6]], writes=[TMb])
                    P.op("dve", lambda e, TM=TM, Z=Z, nk=nk: e.tensor_tensor(out=TM[0:nk, 0:qw], in0=TM[0:nk, 0:qw],
                                                                           in1=Z[0:nk, 0:qw], op=ALU.add),
                         reads=[Sb, TMb], writes=[TMb])
                    P.op("act", lambda e, TM=TM, Pv2=Pv2, nk=nk: e.activation(out=Pv2[0:nk, 0:qw], in_=TM[0:nk, 0:qw], func=AF.Exp),
                         reads=[TMb], writes=[Pb])
                    if kd.get("mask") is not None:
                        mk = kd["mask"]
                        P.op("dve", lambda e, Pv2=Pv2, nk=nk, mk=mk: e.tensor_tensor(
                            out=Pv2[0:nk, 0:qw], in0=Pv2[0:nk, 0:qw], in1=mk, op=ALU.mult),
                            reads=[Pb, buf("CB")], writes=[Pb])

                    def mm_pv(e, Pv2=Pv2, kd=kd, nk=nk, i=i):
                        ins = None
                        for u in range(nsub):
                            ins = e.matmul(ACC[u][0:sw_, 0:256], lhsT=Pv2[0:nk, u * sw_:(u + 1) * sw_],
                                           rhs=kd["V"][0:nk, 0:256], start=(i == 0), stop=(i == nk_tot - 1))
                        return ins
                    P.op("pe", mm_pv, reads=[Pb] + xr, writes=accb)
            qi = state["qt"]
            state["qt"] += 1
            OG = OGT[qi % 2]
            OGb_ = buf("OGT%d" % (qi % 2))
            TP = PS[7][:].bitcast(BF16)[:, 0:512].rearrange("p (c t) -> p c t", c=2)
            j = l // 2
            for u in range(nsub):
                r = slice(0, sw_)
                if is_diff:
                    for m in range(2):
                        P.op("dve", lambda e, m=m, u=u: e.reciprocal(out=SMALL[r, m:m + 1], in_=ACC[m][u][r, 256:257]),
                             reads=accb, writes=[buf("R%d" % m)])
                    P.op("dve", lambda e: e.tensor_tensor(out=SMALL[r, 2:3], in0=SMALL[r, 1:2], in1=NEGLAM[r, j:j + 1], op=ALU.mult),
                         reads=[buf("R1")], writes=[buf("R1l")])
                    P.op("dve", lambda e, u=u: e.tensor_scalar(out=Ot[r, :], in0=ACC[0][u][r, 0:256], scalar1=SMALL[r, 0:1],
                                                                scalar2=None, op0=ALU.mult),
                         reads=accb + [buf("R0")], writes=[buf("O")])
                    P.op("dve", lambda e, u=u: e.scalar_tensor_tensor(out=Ot[r, :], in0=ACC[1][u][r, 0:256], scalar=SMALL[r, 2:3],
                                                                       in1=Ot[r, :], op0=ALU.mult, op1=ALU.add),
                         reads=accb + [buf("R1l"), buf("O")], writes=[buf("O")])
                    P.op("dve", lambda e: e.tensor_tensor(out=O2t[r, :], in0=Ot[r, :], in1=Ot[r, :], op=ALU.mult),
                         reads=[buf("O")], writes=[buf("O2")])
                    P.op("dve", lambda e: e.tensor_reduce(out=SMALL[r, 3:4], in_=O2t[r, :], axis=AX.X, op=ALU.add),
                         reads=[buf("O2")], writes=[buf("SS2")])
                    P.op("act", lambda e: e.activation(out=SMALL[r, 4:5], in_=SMALL[r, 3:4], func=AF.Ln, scale=1.0 / HD,
                                                       bias=EPSB[r, 0:1]),
                         reads=[buf("SS2")], writes=[buf("RS2a")])
                    P.op("act", lambda e: e.activation(out=SMALL[r, 5:6], in_=SMALL[r, 4:5], func=AF.Exp, scale=-0.5),
                         reads=[buf("RS2a")], writes=[buf("RS2")])
                    P.op("dve", lambda e: e.scalar_tensor_tensor(out=Ot[r, :], in0=Ot[r, :], scalar=SMALL[r, 5:6], in1=SW[r, j, :],
                                                                 op0=ALU.mult, op1=ALU.mult),
                         reads=[buf("O"), buf("RS2")], writes=[buf("O")])
                    P.op("dve", lambda e, u=u: e.tensor_tensor(out=OGB[r, :], in0=Ot[r, :], in1=SGv[r, u, :], op=ALU.mult),
                         reads=[buf("O"), SGb], writes=[buf("OGB")])
                else:
                    P.op("dve", lambda e, u=u: e.tensor_tensor(out=OGB[r, :], in0=ACC[u][r, :], in1=SGv[r, u, :], op=ALU.mult),
                         reads=accb + [SGb], writes=[buf("OGB")])

                def tr_o(e, u=u):
                    ins = None
                    for c in range(2):
                        ins = e.transpose(out=TP[:, c, u * sw_:(u + 1) * sw_], in_=OGB[r, c * 128:(c + 1) * 128],
                                          identity=IDENT[r, r])
                    return ins
                P.op("pe", tr_o, reads=[buf("OGB"), buf("CB")], writes=[psb[7]])
            P.op("dve", lambda e: e.tensor_copy(out=OG[:, :, 0:qw], in_=TP[:, :, 0:qw]), reads=[psb[7]], writes=[OGb_])
            ch, off = qd["out"]
            ev = P.dma("pool", ("ogst", qi % 2), ag_src[ch].rearrange("(c p) t -> p c t", p=128)[:, :, off:off + qw],
                       OG[:, :, 0:qw], reads=[OGb_], writes=[buf("agsrc%d" % ch)])

        def a_phase(l):
            is_diff = (l % 2 == 0)
            mbase = 0 if is_diff else 2
            for qt in range(NQT):
                t0 = qt * QW
                sl = qt % 2
                QTb = buf("QTa%d" % sl)
                SGb = buf("SGt%d" % sl)
                P.dma("sp", ("QTa", sl), QTa[sl][:, :, 0:QW], qT[:, :, t0:t0 + QW].rearrange("c p t -> p c t"), writes=[QTb])
                P.dma("sp", ("SGt", sl), SGt[sl][:, :, :], sg[t0:t0 + QW, :].rearrange("(u p) e -> p u e", p=128), writes=[SGb])
                kds = []
                for kb in range(2 * qt + 2):
                    rdiag = kb - 2 * qt
                    kds.append(dict(K=[KT[:, c, kb * 128:(kb + 1) * 128] for c in range(2)], V=VT[:, kb, :], nk=128,
                                    mask=(MASKS[:, mbase + rdiag, :] if rdiag >= 0 else None)))
                if not is_diff:
                    kds = kds[::-1]
                qd = dict(QT=QTa[sl], QTb=QTb, SG=SGt[sl], SGb=SGb, qw=QW, nsub=2, subw=128, out=(t0 // CW, t0 % CW))
                attend(l, qd, kds)
                if (t0 + QW) % CW == 0:
                    ch = t0 // CW
                    P.allgather(ag_src[ch], ag_dst[ch], GROUPS, reads=[buf("agsrc%d" % ch)], writes=[buf("agdst%d" % ch)])
            KTSb, VSb = buf("KTS"), buf("VS")
            TPk = PS[7][:].bitcast(BF16).rearrange("p (j c t) -> p j c t", j=4, c=2)
            for s in range(NS):
                t0 = SEQ + s * DS
                sl = s % 2
                QTb = buf("QTa%d" % sl)
                SGb = buf("SGt%d" % sl)
                P.dma("sp", ("QTa", sl), QTa[sl][:, :, 0:DS], qT[:, :, t0:t0 + DS].rearrange("c p t -> p c t"), writes=[QTb])
                P.dma("sp", ("SGt", sl), SGt[sl][0:DS, 0, :], sg[t0:t0 + DS, :], writes=[SGb])
                if s == 0:
                    P.op("pool", lambda e: e.memset(VS[:, :, 256:257], 1.0), writes=[VSb])
                P.dma("pool", "VS", VS[:, :, 0:256], cv[l, s].rearrange("(j p) e -> p j e", p=128), writes=[VSb])
                for g4 in range(NPB // 4):
                    P.dma("pool", "KPB", KPB[:, :, :], ck[l, s, g4 * 512:(g4 + 1) * 512, :].rearrange("(j p) e -> p j e", p=128),
                          writes=[buf("KPB")])

                    def tr_k(e):
                        ins = None
                        for jj in range(4):
                            for c in range(2):
                                ins = e.transpose(out=TPk[:, jj, c, :], in_=KPB[:, jj, c * 128:(c + 1) * 128], identity=IDENT)
                        return ins
                    P.op("pe", tr_k, reads=[buf("KPB"), buf("CB")], writes=[psb[7]])
                    P.op("act", lambda e, g4=g4: e.activation(
                        out=KTS[:, :, g4 * 512:(g4 + 1) * 512].rearrange("p c (j t) -> p j c t", j=4), in_=TPk, func=AF.Copy),
                        reads=[psb[7]], writes=[KTSb])
                kds = []
                for jb in range(NPB):
                    kds.append(dict(K=[KTS[:, c, jb * 128:(jb + 1) * 128] for c in range(2)], V=VS[:, jb, :], nk=128, mask=None,
                                    reads=[KTSb, VSb]))
                kds.append(dict(K=[KTN[:, c, s * DS:(s + 1) * DS] for c in range(2)], V=VN[:, s, :], nk=DS,
                                mask=(None if is_diff else MSAMP[0:DS, :]), reads=[]))
                if not is_diff:
                    kds = kds[::-1]
                qd = dict(QT=QTa[sl], QTb=QTb, SG=SGt[sl], SGb=SGb, qw=DS, nsub=1, subw=DS, out=(NCH - 1, s * DS))
                attend(l, qd, kds)
            ch = NCH - 1
            P.allgather(ag_src[ch], ag_dst[ch], GROUPS, reads=[buf("agsrc%d" % ch)], writes=[buf("agdst%d" % ch)])
            P.barrier()

        for l in range(DEPTH):
            f_phase(l)
            a_phase(l)
        f_phase(DEPTH)

        @block.tensor
        def _(e):
            for f in P.q["pe"]:
                f(e)

        @block.scalar
        def _(e):
            for f in P.q["act"]:
                f(e)

        @block.vector
        def _(e):
            for f in P.q["dve"]:
                f(e)

        @block.gpsimd
        def _(e):
            for f in P.q["pool"]:
                f(e)

        @block.sync
        def _(e):
            for f in P.q["sp"]:
                f(e)
    return nc


_CACHE = {}


def _consts(SEQ, PAST):
    TT = SEQ + NS * DS
    half = 64
    inv = (10000.0 ** (-np.arange(half, dtype=np.float32) / half)).astype(np.float32)
    pos = np.concatenate([np.arange(SEQ), np.tile(PAST + np.arange(DS), NS)]).astype(np.float32)
    ang = pos[:, None] * inv[None, :]
    cos, sin = np.cos(ang).astype(np.float32), np.sin(ang).astype(np.float32)
    rope = np.concatenate([cos, -sin, sin], axis=1).astype(np.float32)
    k = np.arange(128)[:, None]
    q = np.arange(QW)[None, :]
    cst = np.zeros((128, 1344), np.float32)
    for r in range(2):
        cst[:, r * 256:(r + 1) * 256] = ((128 * r + k) // 64 <= q // 64)
        cst[:, 512 + r * 256:512 + (r + 1) * 256] = ((128 * r + k) < q)
    cst[:, 1024:1088] = (k < np.arange(64)[None, :])
    cst[:, 1088:1216] = np.eye(128)
    cst[:, 1216:1344] = (k < np.arange(128)[None, :])
    return rope, cst


def kernel(x_prompt, x_sample, cache_k, cache_v, norm_w, w_in, w_out, diff_lambda, diff_subln_w, final_norm_w):
    f = lambda a: np.ascontiguousarray(np.asarray(a, dtype=np.float32))
    x_prompt, x_sample, cache_k, cache_v = f(x_prompt), f(x_sample), f(cache_k), f(cache_v)
    norm_w, w_in, w_out = f(norm_w), f(w_in), f(w_out)
    diff_lambda, diff_subln_w, final_norm_w = f(diff_lambda), f(diff_subln_w), f(final_norm_w)
    BATCH, SEQ, _ = x_prompt.shape
    DEPTH, DEC_BATCH, PAST, _ = cache_k.shape
    assert BATCH == 2 and DEC_BATCH == 2 * NS and x_sample.shape[1] == DS
    TT = SEQ + NS * DS
    key = (SEQ, PAST, DEPTH)
    if key not in _CACHE:
        _CACHE[key] = build_program(SEQ, PAST, DEPTH)
    nc = _CACHE[key]
    rope, cst = _consts(SEQ, PAST)
    normw_b = np.ascontiguousarray(np.broadcast_to(
        np.concatenate([norm_w, final_norm_w[None, :]], 0)[:, None, :], (DEPTH + 1, 128, D)))
    nl = diff_lambda.shape[0]
    lam_b = np.zeros((2, 128, 512), np.float32)
    subw_b = np.zeros((2, 128, HD), np.float32)
    lam_b[:nl] = np.broadcast_to(diff_lambda.reshape(nl, 1, 512), (nl, 128, 512))
    subw_b[:nl] = np.broadcast_to(diff_subln_w.reshape(nl, 1, HD), (nl, 128, HD))
    in_maps = []
    for c in range(8):
        b, h = c // 4, c % 4
        cols = np.concatenate([np.arange(j * D + h * HD, j * D + (h + 1) * HD) for j in range(4)])
        in_maps.append({
            "xin": np.ascontiguousarray(np.concatenate([x_prompt[b], x_sample[NS * b:NS * (b + 1)].reshape(NS * DS, D)], 0)),
            "w_in_h": np.ascontiguousarray(w_in[:, :, cols]),
            "w_out": w_out,
            "normw_b": normw_b,
            "lam_b": lam_b,
            "subw_b": subw_b,
            "ck": np.ascontiguousarray(cache_k[:, NS * b:NS * (b + 1), :, h * HD:(h + 1) * HD]),
            "cv": np.ascontiguousarray(cache_v[:, NS * b:NS * (b + 1), :, h * HD:(h + 1) * HD]),
            "rope": rope,
            "cst": cst,
        })
    res = run_bass_kernel_spmd(nc, in_maps, core_ids=list(range(8)))
    y_prompt = np.empty((BATCH, SEQ, D), np.float32)
    y_sample = np.empty((DEC_BATCH, DS, D), np.float32)
    nkp = np.empty((DEPTH, BATCH, SEQ, D), np.float32)
    nvp = np.empty((DEPTH, BATCH, SEQ, D), np.float32)
    nks = np.empty((DEPTH, DEC_BATCH, DS, D), np.float32)
    nvs = np.empty((DEPTH, DEC_BATCH, DS, D), np.float32)
    for c in range(8):
        b, h = c // 4, c % 4
        r = res.results[c]
        if h == 0:
            y_prompt[b] = r["y"][:SEQ]
            y_sample[NS * b:NS * (b + 1)] = r["y"][SEQ:].reshape(NS, DS, D)
        ko, vo = r["kout"], r["vout"]
        nkp[:, b, :, h * HD:(h + 1) * HD] = ko[:, :SEQ]
        nvp[:, b, :, h * HD:(h + 1) * HD] = vo[:, :SEQ]
        nks[:, NS * b:NS * (b + 1), :, h * HD:(h + 1) * HD] = ko[:, SEQ:].reshape(DEPTH, NS, DS, HD)
        nvs[:, NS * b:NS * (b + 1), :, h * HD:(h + 1) * HD] = vo[:, SEQ:].reshape(DEPTH, NS, DS, HD)
    return (y_prompt, y_sample, nkp, nvp, nks, nvs)
```

```python
import math
from contextlib import ExitStack

import numpy as np
import concourse.bass as bass
import concourse.mybir as mybir
from concourse.bass_utils import run_bass_kernel_spmd

F32 = mybir.dt.float32
BF16 = mybir.dt.bfloat16
AF = mybir.ActivationFunctionType
ALU = mybir.AluOpType
AX = mybir.AxisListType

D = 1024
HD = 256
NS = 8
DS = 64
QW = 256
CW = 512
EPS = 1e-6
DIFF_SCALE = 128 ** -0.5
SB_SCALE = 256 ** -0.5


class Ev:
    __slots__ = ("sem", "val", "eng")

    def __init__(self, sem, val, eng):
        self.sem, self.val, self.eng = sem, val, eng


class Buf:
    def __init__(self, name):
        self.name = name
        self.w = None
        self.r = {}


class Prog:
    ENG = ("pe", "act", "dve", "pool", "sp")

    def __init__(self, nc, stack):
        self.nc = nc
        self.stack = stack
        self.q = {e: [] for e in self.ENG}
        self.esem = {e: stack.enter_context(nc.semaphore("es_" + e)) for e in ("pe", "act", "dve", "pool")}
        self.ecnt = {e: 0 for e in self.esem}
        self.dsem = {}
        self.dcnt = {}
        self.ccsem = stack.enter_context(nc.semaphore("cc_sem"))
        self.cccnt = 0
        self.seen = {e: {} for e in self.ENG}

    def _waits(self, eng, reads, writes):
        evs = []
        for b in reads:
            if b.w is not None:
                evs.append(b.w)
        for b in writes:
            if b.w is not None:
                evs.append(b.w)
            evs.extend(b.r.values())
        need = {}
        seen = self.seen[eng]
        for ev in evs:
            if eng == "pe" and ev.eng == "pe":
                continue
            k = id(ev.sem)
            if seen.get(k, 0) >= ev.val:
                continue
            if k not in need or need[k][1] < ev.val:
                need[k] = (ev.sem, ev.val)
        for k, (s, v) in need.items():
            seen[k] = v
        return list(need.values())

    def _commit(self, ev, reads, writes):
        for b in reads:
            k = id(ev.sem)
            b.r[k] = ev
        for b in writes:
            b.w = ev
            b.r = {}

    def op(self, eng, fn, reads=(), writes=()):
        waits = self._waits(eng, reads, writes)
        self.ecnt[eng] += 1
        ev = Ev(self.esem[eng], self.ecnt[eng], eng)

        def run(e, waits=waits, fn=fn, ev=ev):
            for s, v in waits:
                e.wait_ge(s, v)
            fn(e).then_inc(ev.sem, 1)

        self.q[eng].append(run)
        self._commit(ev, reads, writes)
        return ev

    def dma(self, eng, key, out, in_, reads=(), writes=()):
        if key not in self.dsem:
            self.dsem[key] = self.stack.enter_context(self.nc.semaphore("ds%d" % len(self.dsem)))
            self.dcnt[key] = 0
        waits = self._waits(eng, reads, writes)
        self.dcnt[key] += 16
        ev = Ev(self.dsem[key], self.dcnt[key], "dma")

        def run(e, waits=waits, ev=ev, out=out, in_=in_):
            for s, v in waits:
                e.wait_ge(s, v)
            e.dma_start(out=out, in_=in_).then_inc(ev.sem, 16)

        self.q[eng].append(run)
        self._commit(ev, reads, writes)
        return ev

    def allgather(self, src, dst, groups, reads=(), writes=()):
        waits = self._waits("pool", reads, writes)
        self.cccnt += 1
        ev = Ev(self.ccsem, self.cccnt, "cc")

        def run(e, waits=waits, ev=ev):
            for s, v in waits:
                e.wait_ge(s, v)
            e.collective_compute("AllGather", ALU.bypass, replica_groups=groups,
                                 ins=[src], outs=[dst]).then_inc(ev.sem)

        self.q["pool"].append(run)
        self._commit(ev, reads, writes)
        return ev

    def barrier(self):
        evs = [(self.esem[e], self.ecnt[e]) for e in self.esem if self.ecnt[e] > 0]
        evs += [(self.dsem[k], self.dcnt[k]) for k in self.dsem if self.dcnt[k] > 0]
        if self.cccnt > 0:
            evs.append((self.ccsem, self.cccnt))
        for eng in self.ENG:
            need = []
            seen = self.seen[eng]
            for s, v in evs:
                if seen.get(id(s), 0) >= v:
                    continue
                seen[id(s)] = v
                need.append((s, v))

            def run(e, need=need):
                for s, v in need:
                    e.wait_ge(s, v)

            self.q[eng].append(run)


def build_program(SEQ, PAST, DEPTH):
    NPT = SEQ // 128
    TT = SEQ + NS * DS
    NQT = SEQ // QW
    NCH = TT // CW
    NPB = PAST // 128
    NTILE = NPT + NS
    assert SEQ % CW == 0 and PAST % 512 == 0

    nc = bass.Bass("TRN2", target_bir_lowering=False)
    dt = nc.dram_tensor
    xin = dt("xin", [TT, D], F32, kind="ExternalInput").ap()
    w_in = dt("w_in_h", [DEPTH, D, D], F32, kind="ExternalInput").ap()
    w_out = dt("w_out", [DEPTH, D, D], F32, kind="ExternalInput").ap()
    normw = dt("normw_b", [DEPTH + 1, 128, D], F32, kind="ExternalInput").ap()
    lam_in = dt("lam_b", [2, 128, 512], F32, kind="ExternalInput").ap()
    subw_in = dt("subw_b", [2, 128, HD], F32, kind="ExternalInput").ap()
    ck = dt("ck", [DEPTH, NS, PAST, HD], F32, kind="ExternalInput").ap()
    cv = dt("cv", [DEPTH, NS, PAST, HD], F32, kind="ExternalInput").ap()
    rope = dt("rope", [TT, 192], F32, kind="ExternalInput").ap()
    cst = dt("cst", [128, 1344], F32, kind="ExternalInput").ap()
    y_out = dt("y", [TT, D], F32, kind="ExternalOutput").ap()
    kout = dt("kout", [DEPTH, TT, HD], F32, kind="ExternalOutput").ap()
    vout = dt("vout", [DEPTH, TT, HD], F32, kind="ExternalOutput").ap()
    xs = dt("xs", [TT, D], F32, kind="Internal").ap()
    qT = dt("qT", [2, 128, TT], BF16, kind="Internal").ap()
    sg = dt("sg", [TT, HD], F32, kind="Internal").ap()
    ag_src = dt("ag_src", [NCH, 256, CW], BF16, kind="Internal").ap()
    ag_dst = dt("ag_dst", [NCH, 1024, CW], BF16, kind="Internal").ap()
    GROUPS = [[0, 1, 2, 3], [4, 5, 6, 7]]

    stack = ExitStack()
    with stack:
        sb = lambda name, shape, dtp: stack.enter_context(nc.sbuf_tensor(name, shape, dtp))
        KT = sb("KT", [128, 2, SEQ], BF16)
        VT = sb("VT", [128, NPT, 257], BF16)
        KTN = sb("KTN", [128, 2, NS * DS], BF16)
        VN = sb("VN", [128, NS, 257], BF16)
        CB = sb("CB", [128, 1344], BF16)
        NEGONES = sb("NEGONES", [128, 128], BF16)
        SW = sb("SW", [128, 2, HD], F32)
        NEGLAM = sb("NEGLAM", [128, 2], F32)
        SMALL = sb("SMALL", [128, 16], F32)
        ZERO1 = sb("ZERO1", [128, 1], F32)
        EPSB = sb("EPSB", [128, 1], F32)
        A16 = sb("A16", [128, 22272], BF16)
        A32 = sb("A32", [128, 5376], F32)
        PS = [stack.enter_context(nc.psum_tensor("ps%d" % i, [128, 512], F32)) for i in range(8)]
        block = stack.enter_context(nc.Block())

        MASKS = CB[:, 0:1024].rearrange("p (r q) -> p r q", r=4)
        MSAMP = CB[:, 1024:1088]
        IDENT = CB[:, 1088:1216]
        TRI = CB[:, 1216:1344]

        class Carver:
            def __init__(self, t):
                self.t, self.o = t, 0

            def take(self, n):
                v = self.t[:, self.o:self.o + n]
                self.o += n
                assert self.o <= self.t.shape[1], (self.o, self.t.shape)
                return v

        c16 = Carver(A16)
        WI = c16.take(8192).rearrange("p (c n) -> p c n", c=8)
        WO = c16.take(8192).rearrange("p (c n) -> p c n", c=8)
        Gs = [c16.take(1024).rearrange("p (c t) -> p c t", c=8) for _ in range(2)]
        Hb = c16.take(1024)
        HTs = [c16.take(1024).rearrange("p (c t) -> p c t", c=8) for _ in range(2)]
        QKB = c16.take(512)
        QTst = c16.take(256).rearrange("p (c t) -> p c t", c=2)
        c32 = Carver(A32)
        Xs = [c32.take(1024) for _ in range(2)]
        NW = c32.take(1024)
        QK = c32.take(512)
        TMPr = c32.take(512)
        VF = c32.take(256)
        SGs = c32.take(256)
        ROPE = [c32.take(192) for _ in range(2)]
        a16 = Carver(A16)
        QTa = [a16.take(512).rearrange("p (c t) -> p c t", c=2) for _ in range(2)]
        Pt = [a16.take(512).rearrange("p (m t) -> p m t", m=2) for _ in range(3)]
        LKt = [a16.take(256) for _ in range(3)]
        OGB = a16.take(256)
        OGT = [a16.take(512).rearrange("p (c t) -> p c t", c=2) for _ in range(2)]
        KPB = a16.take(1024).rearrange("p (j e) -> p j e", j=4)
        KTS2 = [a16.take(2 * PAST).rearrange("p (c t) -> p c t", c=2) for _ in range(2)]
        VS2 = [a16.take(NPB * 257).rearrange("p (j e) -> p j e", j=NPB) for _ in range(2)]
        a32 = Carver(A32)
        SGt = [a32.take(512).rearrange("p (u e) -> p u e", u=2) for _ in range(2)]
        Et = [a32.take(256) for _ in range(2)]
        TMt = [a32.take(256) for _ in range(2)]
        Ot = a32.take(256)
        O2t = a32.take(256)

        P = Prog(nc, stack)

        B = {}

        def buf(name):
            if name not in B:
                B[name] = Buf(name)
            return B[name]

        psb = [buf("ps%d" % i) for i in range(8)]

        P.dma("pool", "cst", CB[:], cst, writes=[buf("CB")])
        P.op("pool", lambda e: e.memset(NEGONES[:], -1.0), writes=[buf("NEGONES")])
        P.op("pool", lambda e: e.memset(VT[:, :, 256:257], 1.0), writes=[buf("VTones")])
        P.op("pool", lambda e: e.memset(VN[:, :, 256:257], 1.0), writes=[buf("VNones")])
        P.op("pool", lambda e: e.memset(ZERO1[:], 0.0), writes=[buf("ZERO1")])
        P.op("pool", lambda e: e.memset(EPSB[:], EPS), writes=[buf("EPSB")])
        LAMT = A32[:, 0:512].rearrange("p (a d) -> p a d", a=4)
        LJ = A32[:, 512:640]
        for j in range(2):
            if 2 * j >= DEPTH:
                break
            lam_init = 0.8 - 0.6 * math.exp(-0.3 * (2 * j))
            P.dma("sp", "lam", LAMT, lam_in[j].rearrange("p (a d) -> p a d", a=4), writes=[buf("LAMT")])
            P.dma("sp", "subw", SW[:, j, :], subw_in[j], writes=[buf("SWraw")])
            for t in range(2):
                P.op("dve", lambda e, t=t: e.tensor_tensor(out=LJ, in0=LAMT[:, 2 * t, :], in1=LAMT[:, 2 * t + 1, :], op=ALU.mult),
                     reads=[buf("LAMT")], writes=[buf("LJ")])
                P.op("dve", lambda e, t=t: e.tensor_reduce(out=SMALL[:, t:t + 1], in_=LJ, axis=AX.X, op=ALU.add),
                     reads=[buf("LJ")], writes=[buf("SMALL")])
            P.op("act", lambda e: e.activation(out=SMALL[:, 2:4], in_=SMALL[:, 0:2], func=AF.Exp),
                 reads=[buf("SMALL")], writes=[buf("SMALL")])
            P.op("dve", lambda e: e.tensor_tensor(out=SMALL[:, 4:5], in0=SMALL[:, 3:4], in1=SMALL[:, 2:3], op=ALU.subtract),
                 reads=[buf("SMALL")], writes=[buf("SMALL")])
            P.op("dve", lambda e, j=j, li=lam_init: e.tensor_scalar(out=NEGLAM[:, j:j + 1], in0=SMALL[:, 4:5], scalar1=-li, scalar2=None, op0=ALU.add),
                 reads=[buf("SMALL")], writes=[buf("NEGLAM")])
            P.op("dve", lambda e, j=j, li=lam_init: e.tensor_scalar(out=SW[:, j, :], in0=SW[:, j, :], scalar1=1.0 - li, scalar2=None, op0=ALU.mult),
                 reads=[buf("SWraw")], writes=[buf("SWraw")])
        P.barrier()

        def tile_info(ti):
            if ti < NPT:
                return ti * 128, 128
            return SEQ + (ti - NPT) * DS, DS

        def f_phase(l):
            last = (l == DEPTH)
            is_diff = (l % 2 == 0)
            if not last:
                P.dma("pool", "WI", WI, w_in[l].rearrange("(c p) n -> p c n", p=128), writes=[buf("WI")])
            if l > 0:
                P.dma("pool", "WO", WO, w_out[l - 1].rearrange("(c p) n -> p c n", p=128), writes=[buf("WO")])
            P.dma("sp", "NW", NW, normw[l], writes=[buf("NW")])
            Y = [PS[0], PS[1]]
            HTp = PS[2][:].bitcast(BF16).rearrange("p (c t) -> p c t", c=8)
            PJs = [[PS[3], PS[4]], [PS[6], PS[7]]]
            pjb = [[psb[3], psb[4]], [psb[6], psb[7]]]
            QKTp = PS[5][:].bitcast(BF16)[:, 0:512].rearrange("p (c t) -> p c t", c=4)

            def front(ti):
                t0, n = tile_info(ti)
                sl = ti % 2
                HT = HTs[sl]
                HTb = buf("HT%d" % sl)
                X = Xs[sl]
                Xb = buf("X%d" % sl)
                P.dma("sp", ("X", sl), X[0:n, :], (xin if l == 0 else xs)[t0:t0 + n, :], writes=[Xb])
                if l > 0:
                    G = Gs[sl]
                    Gb = buf("G%d" % sl)
                    ch, off = t0 // CW, t0 % CW
                    P.dma("sp", ("G", sl), G[:, :, 0:n],
                          ag_dst[ch].rearrange("(c p) t -> p c t", p=128)[:, :, off:off + n], writes=[Gb])

                    def mm_out(e, G=G, n=n):
                        ins = None
                        for nn in range(2):
                            for c in range(8):
                                ins = e.matmul(Y[nn][0:n, :], lhsT=G[:, c, 0:n], rhs=WO[:, c, nn * 512:(nn + 1) * 512],
                                               start=(c == 0), stop=(c == 7))
                        return ins
                    P.op("pe", mm_out, reads=[Gb, buf("WO")], writes=[psb[0], psb[1]])
                    for nn in range(2):
                        P.op("dve", lambda e, X=X, n=n, nn=nn: e.tensor_tensor(
                            out=X[0:n, nn * 512:(nn + 1) * 512], in0=X[0:n, nn * 512:(nn + 1) * 512],
                            in1=Y[nn][0:n, :], op=ALU.add), reads=[psb[nn], Xb], writes=[Xb])
                if not last:
                    P.dma("pool", ("Xst", sl), xs[t0:t0 + n, :], X[0:n, :], reads=[Xb])
                SSb = buf("SS")
                P.op("act", lambda e, X=X, n=n: e.activation(out=Hb[0:n, :], in_=X[0:n, :], func=AF.Square,
                                                              accum_out=SMALL[0:n, 8:9]),
                     reads=[Xb], writes=[buf("H"), SSb])
                P.op("act", lambda e, n=n: e.activation(out=SMALL[0:n, 9:10], in_=SMALL[0:n, 8:9], func=AF.Ln,
                                                        scale=1.0 / D, bias=EPSB[0:n, 0:1]),
                     reads=[SSb], writes=[buf("RS0")])
                P.op("act", lambda e, n=n: e.activation(out=SMALL[0:n, 10:11], in_=SMALL[0:n, 9:10], func=AF.Exp, scale=-0.5),
                     reads=[buf("RS0")], writes=[buf("RS")])
                if last:
                    P.op("dve", lambda e, X=X, n=n: e.scalar_tensor_tensor(
                        out=X[0:n, :], in0=X[0:n, :], scalar=SMALL[0:n, 10:11], in1=NW[0:n, :], op0=ALU.mult, op1=ALU.mult),
                        reads=[Xb, buf("RS"), buf("NW")], writes=[Xb])
                    P.dma("pool", ("Yst", sl), y_out[t0:t0 + n, :], X[0:n, :], reads=[Xb])
                    return
                P.op("dve", lambda e, X=X, n=n: e.scalar_tensor_tensor(
                    out=Hb[0:n, :], in0=X[0:n, :], scalar=SMALL[0:n, 10:11], in1=NW[0:n, :], op0=ALU.mult, op1=ALU.mult),
                    reads=[Xb, buf("RS"), buf("NW")], writes=[buf("H")])

                def tr_h(e, n=n):
                    ins = None
                    for c in range(8):
                        ins = e.transpose(out=HTp[:, c, 0:n], in_=Hb[0:n, c * 128:(c + 1) * 128], identity=IDENT[0:n, 0:n])
                    return ins
                P.op("pe", tr_h, reads=[buf("H"), buf("CB")], writes=[psb[2]])
                P.op("act", lambda e, n=n: e.activation(out=HT[:, :, 0:n], in_=HTp[:, :, 0:n], func=AF.Copy),
                     reads=[psb[2]], writes=[HTb])

            def back(ti):
                t0, n = tile_info(ti)
                sl = ti % 2
                HT = HTs[sl]
                HTb = buf("HT%d" % sl)
                PJ = PJs[sl]
                pb3, pb4 = pjb[sl]

                def mm_in(e, n=n):
                    ins = None
                    for nn in range(2):
                        for c in range(8):
                            ins = e.matmul(PJ[nn][0:n, :], lhsT=HT[:, c, 0:n], rhs=WI[:, c, nn * 512:(nn + 1) * 512],
                                           start=(c == 0), stop=(c == 7))
                    return ins
                P.op("pe", mm_in, reads=[HTb, buf("WI")], writes=[pb3, pb4])
                QKb = buf("QK")
                if is_diff:
                    R = ROPE[sl]
                    Rb = buf("ROPE%d" % sl)
                    P.dma("sp", ("ROPE", sl), R[0:n, :], rope[t0:t0 + n, :], writes=[Rb])
                    pj8 = PJ[0][0:n, :].rearrange("p (m d) -> p m d", m=8)
                    pj4 = PJ[0][0:n, :].rearrange("p (m h d) -> p m h d", m=4, h=2)
                    tm4 = TMPr[0:n, :].rearrange("p (m h d) -> p m h d", m=4, h=2)
                    P.op("dve", lambda e, n=n, R=R, pj8=pj8: e.tensor_tensor(
                        out=QK[0:n, :].rearrange("p (m d) -> p m d", m=8), in0=pj8,
                        in1=R[0:n, 0:64].unsqueeze(1).broadcast_to([n, 8, 64]), op=ALU.mult),
                        reads=[pb3, Rb], writes=[QKb])
                    P.op("dve", lambda e, n=n, R=R, pj4=pj4, tm4=tm4: e.tensor_tensor(
                        out=tm4[:, :, 0, :], in0=pj4[:, :, 1, :],
                        in1=R[0:n, 64:128].unsqueeze(1).broadcast_to([n, 4, 64]), op=ALU.mult),
                        reads=[pb3, Rb], writes=[buf("TMPa")])
                    P.op("dve", lambda e, n=n, R=R, pj4=pj4, tm4=tm4: e.tensor_tensor(
                        out=tm4[:, :, 1, :], in0=pj4[:, :, 0, :],
                        in1=R[0:n, 128:192].unsqueeze(1).broadcast_to([n, 4, 64]), op=ALU.mult),
                        reads=[pb3, Rb], writes=[buf("TMPb")])
                    P.op("dve", lambda e, n=n: e.tensor_tensor(out=QK[0:n, :], in0=QK[0:n, :], in1=TMPr[0:n, :], op=ALU.add),
                         reads=[QKb, buf("TMPa"), buf("TMPb")], writes=[QKb])
                else:
                    P.op("dve", lambda e, n=n: e.tensor_copy(out=QK[0:n, :], in_=PJ[0][0:n, :]),
                         reads=[pb3], writes=[QKb])
                P.dma("pool", "kst", kout[l, t0:t0 + n, :], QK[0:n, 256:512], reads=[QKb])
                qs = 1.0 if is_diff else SB_SCALE
                P.op("act", lambda e, n=n, qs=qs: e.activation(out=QKB[0:n, 0:256], in_=QK[0:n, 0:256], func=AF.Copy, scale=qs),
                     reads=[QKb], writes=[buf("QKBq")])
                P.op("act", lambda e, n=n: e.activation(out=QKB[0:n, 256:512], in_=QK[0:n, 256:512], func=AF.Copy),
                     reads=[QKb], writes=[buf("QKBk")])

                def tr_qk(e, n=n):
                    ins = None
                    for c in range(4):
                        ins = e.transpose(out=QKTp[:, c, 0:n], in_=QKB[0:n, c * 128:(c + 1) * 128], identity=IDENT[0:n, 0:n])
                    return ins
                P.op("pe", tr_qk, reads=[buf("QKBq"), buf("QKBk")], writes=[psb[5]])
                P.op("act", lambda e, n=n: e.activation(out=QTst[:, :, 0:n], in_=QKTp[:, 0:2, 0:n], func=AF.Copy),
                     reads=[psb[5]], writes=[buf("QTst")])
                if ti < NPT:
                    kdst = KT[:, :, t0:t0 + n]
                    vdst = VT[0:n, ti, 0:256]
                else:
                    s = ti - NPT
                    kdst = KTN[:, :, s * DS:(s + 1) * DS]
                    vdst = VN[0:n, s, 0:256]
                P.op("act", lambda e, n=n, kdst=kdst: e.activation(out=kdst, in_=QKTp[:, 2:4, 0:n], func=AF.Copy),
                     reads=[psb[5]], writes=[])
                P.dma("pool", "qst", qT[:, :, t0:t0 + n].rearrange("c p t -> p c t"), QTst[:, :, 0:n], reads=[buf("QTst")])
                P.op("dve", lambda e, n=n: e.tensor_copy(out=VF[0:n, :], in_=PJ[1][0:n, 0:256]),
                     reads=[pb4], writes=[buf("VF")])
                P.op("dve", lambda e, n=n, vdst=vdst: e.tensor_copy(out=vdst, in_=PJ[1][0:n, 0:256]),
                     reads=[pb4], writes=[])
                P.dma("pool", "vst", vout[l, t0:t0 + n, :], VF[0:n, :], reads=[buf("VF")])
                P.op("act", lambda e, n=n: e.activation(out=SGs[0:n, :], in_=PJ[1][0:n, 256:512], func=AF.Silu),
                     reads=[pb4], writes=[buf("SGs")])
                P.dma("pool", "gst", sg[t0:t0 + n, :], SGs[0:n, :], reads=[buf("SGs")])
            front(0)
            for ti in range(NTILE):
                if ti + 1 < NTILE:
                    front(ti + 1)
                if not last:
                    back(ti)
            P.barrier()

        state = {"step": 0, "qt": 0}

        def a_phase(l):
            is_diff = (l % 2 == 0)
            mbase = 0 if is_diff else 2
            jl = l // 2
            tiles = [("p", qt) for qt in range(NQT)] + [("s", s) for s in range(NS)]
            NTL = len(tiles)
            TP = PS[7][:].bitcast(BF16)[:, 0:512].rearrange("p (c t) -> p c t", c=2)
            TPk = PS[7][:].bitcast(BF16).rearrange("p (j c t) -> p j c t", j=4, c=2)
            if is_diff:
                SBK = [0, 1]
                ACC = [[PS[2 + 2 * m + u] for u in range(2)] for m in range(2)]
                accb = [psb[2], psb[3], psb[4], psb[5]]
            else:
                SBK = [0, 1, 3]
                ACC = [PS[2][:, 0:256], PS[2][:, 256:512]]
                accb = [psb[2]]
                CACC = PS[6]

            def describe(ti):
                kind, idx = tiles[ti]
                sl = ti % 2
                QTb, SGb = buf("QTa%d" % sl), buf("SGt%d" % sl)
                if kind == "p":
                    qt = idx
                    t0 = qt * QW
                    kds = []
                    for kb in range(2 * qt + 2):
                        rdiag = kb - 2 * qt
                        kds.append(dict(K=[KT[:, c, kb * 128:(kb + 1) * 128] for c in range(2)], V=VT[:, kb, :], nk=128,
                                        mask=(MASKS[:, mbase + rdiag, :] if rdiag >= 0 else None), reads=[]))
                    qd = dict(QT=QTa[sl], QTb=QTb, SG=SGt[sl], SGb=SGb, qw=QW, nsub=2, subw=128,
                              out=(t0 // CW, t0 % CW), ag=((t0 + QW) % CW == 0))
                else:
                    s = idx
                    KTSb, VSb = buf("KTS%d" % sl), buf("VS%d" % sl)
                    kds = []
                    for jb in range(NPB):
                        kds.append(dict(K=[KTS2[sl][:, c, jb * 128:(jb + 1) * 128] for c in range(2)], V=VS2[sl][:, jb, :],
                                        nk=128, mask=None, reads=[KTSb, VSb]))
                    kds.append(dict(K=[KTN[:, c, s * DS:(s + 1) * DS] for c in range(2)], V=VN[:, s, :], nk=DS,
                                    mask=(None if is_diff else MSAMP[0:DS, :]), reads=[]))
                    qd = dict(QT=QTa[sl], QTb=QTb, SG=SGt[sl], SGb=SGb, qw=DS, nsub=1, subw=DS,
                              out=(NCH - 1, s * DS), ag=(s == NS - 1))
                if not is_diff:
                    kds = kds[::-1]
                return qd, kds

            desc = [describe(ti) for ti in range(NTL)]

            def emit_prep(ti):
                kind, idx = tiles[ti]
                sl = ti % 2
                QTb, SGb = buf("QTa%d" % sl), buf("SGt%d" % sl)
                if kind == "p":
                    t0 = idx * QW
                    P.dma("sp", ("QTa", sl), QTa[sl][:, :, 0:QW], qT[:, :, t0:t0 + QW].rearrange("c p t -> p c t"), writes=[QTb])
                    P.dma("sp", ("SGt", sl), SGt[sl][:, :, :], sg[t0:t0 + QW, :].rearrange("(u p) e -> p u e", p=128), writes=[SGb])
                    return
                s = idx
                t0 = SEQ + s * DS
                KTSs, VSs = KTS2[sl], VS2[sl]
                KTSb, VSb = buf("KTS%d" % sl), buf("VS%d" % sl)
                P.dma("sp", ("QTa", sl), QTa[sl][:, :, 0:DS], qT[:, :, t0:t0 + DS].rearrange("c p t -> p c t"), writes=[QTb])
                P.dma("sp", ("SGt", sl), SGt[sl][0:DS, 0, :], sg[t0:t0 + DS, :], writes=[SGb])
                P.op("pool", lambda e, VSs=VSs: e.memset(VSs[:, :, 256:257], 1.0), writes=[VSb])
                P.dma("pool", ("VS", sl), VSs[:, :, 0:256], cv[l, s].rearrange("(j p) e -> p j e", p=128), writes=[VSb])
                for g4 in range(NPB // 4):
                    P.dma("pool", "KPB", KPB[:, :, :], ck[l, s, g4 * 512:(g4 + 1) * 512, :].rearrange("(j p) e -> p j e", p=128),
                          writes=[buf("KPB")])

                    def tr_k(e):
                        ins = None
                        for jj in range(4):
                            for c in range(2):
                                ins = e.transpose(out=TPk[:, jj, c, :], in_=KPB[:, jj, c * 128:(c + 1) * 128], identity=IDENT)
                        return ins
                    P.op("pe", tr_k, reads=[buf("KPB"), buf("CB")], writes=[psb[7]])
                    P.op("act", lambda e, g4=g4, KTSs=KTSs: e.activation(
                        out=KTSs[:, :, g4 * 512:(g4 + 1) * 512].rearrange("p c (j t) -> p j c t", j=4), in_=TPk, func=AF.Copy),
                        reads=[psb[7]], writes=[KTSb])

            flat = [(ti, i) for ti in range(NTL) for i in range(len(desc[ti][1]))]
            base = state["step"]
            state["step"] += len(flat)

            def ctx(fi):
                ti, i = flat[fi]
                qd, kds = desc[ti]
                return ti, i, qd, kds[i], len(kds), base + fi

            def d_stage0(fi):
                ti, i, qd, kd, nkt, st = ctx(fi)
                qw, nk, QT = qd["qw"], kd["nk"], qd["QT"]
                Sb = psb[SBK[st % 2]]
                S = PS[SBK[st % 2]][:].rearrange("p (m t) -> p m t", m=2)
                Pv, Pb = Pt[st % 3], buf("P%d" % (st % 3))

                def mm_qk(e):
                    ins = None
                    for m in range(2):
                        ins = e.matmul(S[0:nk, m, 0:qw], lhsT=kd["K"][m], rhs=QT[:, m, 0:qw], start=True, stop=True)
                    return ins
                P.op("pe", mm_qk, reads=[qd["QTb"]] + kd["reads"], writes=[Sb])
                P.op("act", lambda e: e.activation(out=Pv[0:nk, :, 0:qw], in_=S[0:nk, :, 0:qw], func=AF.Exp, scale=DIFF_SCALE),
                     reads=[Sb], writes=[Pb])
                if kd["mask"] is not None:
                    mk = kd["mask"]
                    P.op("dve", lambda e: e.tensor_tensor(out=Pv[0:nk, :, 0:qw], in0=Pv[0:nk, :, 0:qw],
                                                          in1=mk.unsqueeze(1).broadcast_to([nk, 2, qw]), op=ALU.mult),
                         reads=[Pb, buf("CB")], writes=[Pb])

            def d_stage1(fi):
                ti, i, qd, kd, nkt, st = ctx(fi)
                qw, nk, nsub, sw_ = qd["qw"], kd["nk"], qd["nsub"], qd["subw"]
                Pv, Pb = Pt[st % 3], buf("P%d" % (st % 3))

                def mm_pv(e):
                    ins = None
                    for m in range(2):
                        for u in range(nsub):
                            ins = e.matmul(ACC[m][u][0:sw_, 0:257], lhsT=Pv[0:nk, m, u * sw_:(u + 1) * sw_],
                                           rhs=kd["V"][0:nk, 0:257], start=(i == 0), stop=(i == nkt - 1))
                    return ins
                P.op("pe", mm_pv, reads=[Pb] + kd["reads"], writes=accb)
                if i == nkt - 1:
                    epilogue(ti)

            def s_stage0(fi):
                ti, i, qd, kd, nkt, st = ctx(fi)
                qw, nk, QT = qd["qw"], kd["nk"], qd["QT"]
                Sb, Z = psb[SBK[st % 3]], PS[SBK[st % 3]]
                E, Eb = Et[st % 2], buf("E%d" % (st % 2))
                LK, LKb = LKt[st % 3], buf("LK%d" % (st % 3))

                def mm_z(e):
                    ins = None
                    for c in range(2):
                        ins = e.matmul(Z[0:nk, 0:qw], lhsT=kd["K"][c], rhs=QT[:, c, 0:qw], start=(c == 0), stop=False)
                    return ins
                P.op("pe", mm_z, reads=[qd["QTb"]] + kd["reads"], writes=[Sb])
                P.op("act", lambda e: e.activation(out=E[0:nk, 0:qw], in_=Z[0:nk, 0:qw], func=AF.Exp), reads=[Sb], writes=[Eb])
                P.op("act", lambda e: e.activation(out=LK[0:nk, 0:qw], in_=E[0:nk, 0:qw], func=AF.Ln, bias=1.0),
                     reads=[Eb], writes=[LKb])
                if kd["mask"] is not None:
                    mk = kd["mask"]
                    P.op("dve", lambda e: e.tensor_tensor(out=LK[0:nk, 0:qw], in0=LK[0:nk, 0:qw], in1=mk, op=ALU.mult),
                         reads=[LKb, buf("CB")], writes=[LKb])

            def s_stage1(fi):
                ti, i, qd, kd, nkt, st = ctx(fi)
                qw, nk = qd["qw"], kd["nk"]
                Sb, Z = psb[SBK[st % 3]], PS[SBK[st % 3]]
                LK, LKb = LKt[st % 3], buf("LK%d" % (st % 3))
                TM, TMb = TMt[st % 2], buf("TM%d" % (st % 2))
                Pv2, Pb = Pt[st % 3][:, 0, :], buf("P%d" % (st % 3))

                def mm_tri(e):
                    e.matmul(Z[0:nk, 0:qw], lhsT=TRI[0:nk, 0:nk], rhs=LK[0:nk, 0:qw], start=False, stop=True)
                    return e.matmul(CACC[:, 0:qw], lhsT=NEGONES[0:nk, :], rhs=LK[0:nk, 0:qw],
                                    start=(i == 0), stop=(i == nkt - 1))
                P.op("pe", mm_tri, reads=[LKb, buf("CB"), buf("NEGONES")], writes=[Sb, psb[6]])
                P.op("dve", lambda e: e.tensor_copy(out=TM[0:nk, 0:qw], in_=CACC[0:nk, 0:qw]), reads=[psb[6]], writes=[TMb])
                P.op("dve", lambda e: e.tensor_tensor(out=TM[0:nk, 0:qw], in0=TM[0:nk, 0:qw], in1=Z[0:nk, 0:qw], op=ALU.add),
                     reads=[Sb, TMb], writes=[TMb])

            def s_stage1b(fi):
                ti, i, qd, kd, nkt, st = ctx(fi)
                qw, nk = qd["qw"], kd["nk"]
                TM, TMb = TMt[st % 2], buf("TM%d" % (st % 2))
                Pv2, Pb = Pt[st % 3][:, 0, :], buf("P%d" % (st % 3))
                P.op("act", lambda e: e.activation(out=Pv2[0:nk, 0:qw], in_=TM[0:nk, 0:qw], func=AF.Exp), reads=[TMb], writes=[Pb])
                if kd["mask"] is not None:
                    mk = kd["mask"]
                    P.op("dve", lambda e: e.tensor_tensor(out=Pv2[0:nk, 0:qw], in0=Pv2[0:nk, 0:qw], in1=mk, op=ALU.mult),
                         reads=[Pb, buf("CB")], writes=[Pb])

            def s_stage2(fi):
                ti, i, qd, kd, nkt, st = ctx(fi)
                qw, nk, nsub, sw_ = qd["qw"], kd["nk"], qd["nsub"], qd["subw"]
                Pv2, Pb = Pt[st % 3][:, 0, :], buf("P%d" % (st % 3))

                def mm_pv(e):
                    ins = None
                    for u in range(nsub):
                        ins = e.matmul(ACC[u][0:sw_, 0:256], lhsT=Pv2[0:nk, u * sw_:(u + 1) * sw_],
                                       rhs=kd["V"][0:nk, 0:256], start=(i == 0), stop=(i == nkt - 1))
                    return ins
                P.op("pe", mm_pv, reads=[Pb] + kd["reads"], writes=accb)
                if i == nkt - 1:
                    epilogue(ti)

            def epilogue(ti):
                qd, kds = desc[ti]
                qw, nsub, sw_ = qd["qw"], qd["nsub"], qd["subw"]
                SGv, SGb = qd["SG"], qd["SGb"]
                qi = state["qt"]
                state["qt"] += 1
                OG = OGT[qi % 2]
                OGb_ = buf("OGT%d" % (qi % 2))
                r = slice(0, sw_)
                for u in range(nsub):
                    if is_diff:
                        for m in range(2):
                            P.op("dve", lambda e, m=m, u=u: e.reciprocal(out=SMALL[r, m:m + 1], in_=ACC[m][u][r, 256:257]),
                                 reads=accb, writes=[buf("R%d" % m)])
                        P.op("dve", lambda e: e.tensor_tensor(out=SMALL[r, 2:3], in0=SMALL[r, 1:2], in1=NEGLAM[r, jl:jl + 1], op=ALU.mult),
                             reads=[buf("R1")], writes=[buf("R1l")])
                        P.op("dve", lambda e, u=u: e.tensor_scalar(out=Ot[r, :], in0=ACC[0][u][r, 0:256], scalar1=SMALL[r, 0:1],
                                                                    scalar2=None, op0=ALU.mult),
                             reads=accb + [buf("R0")], writes=[buf("O")])
                        P.op("dve", lambda e, u=u: e.scalar_tensor_tensor(out=Ot[r, :], in0=ACC[1][u][r, 0:256], scalar=SMALL[r, 2:3],
                                                                           in1=Ot[r, :], op0=ALU.mult, op1=ALU.add),
                             reads=accb + [buf("R1l"), buf("O")], writes=[buf("O")])
                        P.op("dve", lambda e: e.tensor_tensor(out=O2t[r, :], in0=Ot[r, :], in1=Ot[r, :], op=ALU.mult),
                             reads=[buf("O")], writes=[buf("O2")])
                        P.op("dve", lambda e: e.tensor_reduce(out=SMALL[r, 3:4], in_=O2t[r, :], axis=AX.X, op=ALU.add),
                             reads=[buf("O2")], writes=[buf("SS2")])
                        P.op("act", lambda e: e.activation(out=SMALL[r, 4:5], in_=SMALL[r, 3:4], func=AF.Ln, scale=1.0 / HD,
                                                           bias=EPSB[r, 0:1]),
                             reads=[buf("SS2")], writes=[buf("RS2a")])
                        P.op("act", lambda e: e.activation(out=SMALL[r, 5:6], in_=SMALL[r, 4:5], func=AF.Exp, scale=-0.5),
                             reads=[buf("RS2a")], writes=[buf("RS2")])
                        P.op("dve", lambda e: e.scalar_tensor_tensor(out=Ot[r, :], in0=Ot[r, :], scalar=SMALL[r, 5:6], in1=SW[r, jl, :],
                                                                     op0=ALU.mult, op1=ALU.mult),
                             reads=[buf("O"), buf("RS2")], writes=[buf("O")])
                        P.op("dve", lambda e, u=u: e.tensor_tensor(out=OGB[r, :], in0=Ot[r, :], in1=SGv[r, u, :], op=ALU.mult),
                             reads=[buf("O"), SGb], writes=[buf("OGB")])
                    else:
                        P.op("dve", lambda e, u=u: e.tensor_tensor(out=OGB[r, :], in0=ACC[u][r, :], in1=SGv[r, u, :], op=ALU.mult),
                             reads=accb + [SGb], writes=[buf("OGB")])

                    def tr_o(e, u=u):
                        ins = None
                        for c in range(2):
                            ins = e.transpose(out=TP[:, c, u * sw_:(u + 1) * sw_], in_=OGB[r, c * 128:(c + 1) * 128],
                                              identity=IDENT[r, r])
                        return ins
                    P.op("pe", tr_o, reads=[buf("OGB"), buf("CB")], writes=[psb[7]])
                P.op("dve", lambda e: e.tensor_copy(out=OG[:, :, 0:qw], in_=TP[:, :, 0:qw]), reads=[psb[7]], writes=[OGb_])
                ch, off = qd["out"]
                P.dma("pool", ("ogst", qi % 2), ag_src[ch].rearrange("(c p) t -> p c t", p=128)[:, :, off:off + qw],
                      OG[:, :, 0:qw], reads=[OGb_], writes=[buf("agsrc%d" % ch)])
                if qd["ag"]:
                    P.allgather(ag_src[ch], ag_dst[ch], GROUPS, reads=[buf("agsrc%d" % ch)], writes=[buf("agdst%d" % ch)])
                if ti + 2 < NTL:
                    emit_prep(ti + 2)

            stages = [d_stage0, d_stage1] if is_diff else [s_stage0, s_stage1, s_stage1b, s_stage2]
            NSTG = len(stages)
            emit_prep(0)
            emit_prep(1)
            for k in range(-(NSTG - 1), len(flat)):
                for sg_i in range(NSTG):
                    fi = k + (NSTG - 1 - sg_i)
                    if 0 <= fi < len(flat):
                        stages[sg_i](fi)
            P.barrier()

        for l in range(DEPTH):
            f_phase(l)
            a_phase(l)
        f_phase(DEPTH)

        @block.tensor
        def _(e):
            for f in P.q["pe"]:
                f(e)

        @block.scalar
        def _(e):
            for f in P.q["act"]:
                f(e)

        @block.vector
        def _(e):
            for f in P.q["dve"]:
                f(e)

        @block.gpsimd
        def _(e):
            for f in P.q["pool"]:
                f(e)

        @block.sync
        def _(e):
            for f in P.q["sp"]:
                f(e)
    return nc


_CACHE = {}


def _consts(SEQ, PAST):
    TT = SEQ + NS * DS
    half = 64
    inv = (10000.0 ** (-np.arange(half, dtype=np.float32) / half)).astype(np.float32)
    pos = np.concatenate([np.arange(SEQ), np.tile(PAST + np.arange(DS), NS)]).astype(np.float32)
    ang = pos[:, None] * inv[None, :]
    cos, sin = np.cos(ang).astype(np.float32), np.sin(ang).astype(np.float32)
    rope = np.concatenate([cos, -sin, sin], axis=1).astype(np.float32)
    k = np.arange(128)[:, None]
    q = np.arange(QW)[None, :]
    cst = np.zeros((128, 1344), np.float32)
    for r in range(2):
        cst[:, r * 256:(r + 1) * 256] = ((128 * r + k) // 64 <= q // 64)
        cst[:, 512 + r * 256:512 + (r + 1) * 256] = ((128 * r + k) < q)
    cst[:, 1024:1088] = (k < np.arange(64)[None, :])
    cst[:, 1088:1216] = np.eye(128)
    cst[:, 1216:1344] = (k < np.arange(128)[None, :])
    return rope, cst


def kernel(x_prompt, x_sample, cache_k, cache_v, norm_w, w_in, w_out, diff_lambda, diff_subln_w, final_norm_w):
    f = lambda a: np.ascontiguousarray(np.asarray(a, dtype=np.float32))
    x_prompt, x_sample, cache_k, cache_v = f(x_prompt), f(x_sample), f(cache_k), f(cache_v)
    norm_w, w_in, w_out = f(norm_w), f(w_in), f(w_out)
    diff_lambda, diff_subln_w, final_norm_w = f(diff_lambda), f(diff_subln_w), f(final_norm_w)
    BATCH, SEQ, _ = x_prompt.shape
    DEPTH, DEC_BATCH, PAST, _ = cache_k.shape
    assert BATCH == 2 and DEC_BATCH == 2 * NS and x_sample.shape[1] == DS
    TT = SEQ + NS * DS
    key = (SEQ, PAST, DEPTH)
    if key not in _CACHE:
        _CACHE[key] = build_program(SEQ, PAST, DEPTH)
    nc = _CACHE[key]
    rope, cst = _consts(SEQ, PAST)
    normw_b = np.ascontiguousarray(np.broadcast_to(
        np.concatenate([norm_w, final_norm_w[None, :]], 0)[:, None, :], (DEPTH + 1, 128, D)))
    nl = diff_lambda.shape[0]
    lam_b = np.zeros((2, 128, 512), np.float32)
    subw_b = np.zeros((2, 128, HD), np.float32)
    lam_b[:nl] = np.broadcast_to(diff_lambda.reshape(nl, 1, 512), (nl, 128, 512))
    subw_b[:nl] = np.broadcast_to(diff_subln_w.reshape(nl, 1, HD), (nl, 128, HD))
    in_maps = []
    for c in range(8):
        b, h = c // 4, c % 4
        cols = np.concatenate([np.arange(j * D + h * HD, j * D + (h + 1) * HD) for j in range(4)])
        in_maps.append({
            "xin": np.ascontiguousarray(np.concatenate([x_prompt[b], x_sample[NS * b:NS * (b + 1)].reshape(NS * DS, D)], 0)),
            "w_in_h": np.ascontiguousarray(w_in[:, :, cols]),
            "w_out": w_out,
            "normw_b": normw_b,
            "lam_b": lam_b,
            "subw_b": subw_b,
            "ck": np.ascontiguousarray(cache_k[:, NS * b:NS * (b + 1), :, h * HD:(h + 1) * HD]),
            "cv": np.ascontiguousarray(cache_v[:, NS * b:NS * (b + 1), :, h * HD:(h + 1) * HD]),
            "rope": rope,
            "cst": cst,
        })
    res = run_bass_kernel_spmd(nc, in_maps, core_ids=list(range(8)))
    y_prompt = np.empty((BATCH, SEQ, D), np.float32)
    y_sample = np.empty((DEC_BATCH, DS, D), np.float32)
    nkp = np.empty((DEPTH, BATCH, SEQ, D), np.float32)
    nvp = np.empty((DEPTH, BATCH, SEQ, D), np.float32)
    nks = np.empty((DEPTH, DEC_BATCH, DS, D), np.float32)
    nvs = np.empty((DEPTH, DEC_BATCH, DS, D), np.float32)
    for c in range(8):
        b, h = c // 4, c % 4
        r = res.results[c]
        if h == 0:
            y_prompt[b] = r["y"][:SEQ]
            y_sample[NS * b:NS * (b + 1)] = r["y"][SEQ:].reshape(NS, DS, D)
        ko, vo = r["kout"], r["vout"]
        nkp[:, b, :, h * HD:(h + 1) * HD] = ko[:, :SEQ]
        nvp[:, b, :, h * HD:(h + 1) * HD] = vo[:, :SEQ]
        nks[:, NS * b:NS * (b + 1), :, h * HD:(h + 1) * HD] = ko[:, SEQ:].reshape(DEPTH, NS, DS, HD)
        nvs[:, NS * b:NS * (b + 1), :, h * HD:(h + 1) * HD] = vo[:, SEQ:].reshape(DEPTH, NS, DS, HD)
    return (y_prompt, y_sample, nkp, nvp, nks, nvs)
```

```python
import math
from contextlib import ExitStack

import numpy as np
import concourse.bass as bass
import concourse.mybir as mybir
from concourse.bass_utils import run_bass_kernel_spmd

F32 = mybir.dt.float32
BF16 = mybir.dt.bfloat16
AF = mybir.ActivationFunctionType
ALU = mybir.AluOpType
AX = mybir.AxisListType

D = 1024
HD = 256
NS = 8
DS = 64
QW = 256
CW = 512
EPS = 1e-6
DIFF_SCALE = 128 ** -0.5
SB_SCALE = 256 ** -0.5


class Ev:
    __slots__ = ("sem", "val", "eng")

    def __init__(self, sem, val, eng):
        self.sem, self.val, self.eng = sem, val, eng


class Buf:
    def __init__(self, name):
        self.name = name
        self.w = None
        self.r = {}


class Prog:
    ENG = ("pe", "act", "dve", "pool", "sp")

    def __init__(self, nc, stack):
        self.nc = nc
        self.stack = stack
        self.q = {e: [] for e in self.ENG}
        self.esem = {e: stack.enter_context(nc.semaphore("es_" + e)) for e in ("pe", "act", "dve", "pool")}
        self.ecnt = {e: 0 for e in self.esem}
        self.dsem = {}
        self.dcnt = {}
        self.ccsem = stack.enter_context(nc.semaphore("cc_sem"))
        self.cccnt = 0
        self.seen = {e: {} for e in self.ENG}

    def _waits(self, eng, reads, writes):
        evs = []
        for b in reads:
            if b.w is not None:
                evs.append(b.w)
        for b in writes:
            if b.w is not None:
                evs.append(b.w)
            evs.extend(b.r.values())
        need = {}
        seen = self.seen[eng]
        for ev in evs:
            if eng == "pe" and ev.eng == "pe":
                continue
            k = id(ev.sem)
            if seen.get(k, 0) >= ev.val:
                continue
            if k not in need or need[k][1] < ev.val:
                need[k] = (ev.sem, ev.val)
        for k, (s, v) in need.items():
            seen[k] = v
        return list(need.values())

    def _commit(self, ev, reads, writes):
        for b in reads:
            k = id(ev.sem)
            b.r[k] = ev
        for b in writes:
            b.w = ev
            b.r = {}

    def op(self, eng, fn, reads=(), writes=()):
        waits = self._waits(eng, reads, writes)
        self.ecnt[eng] += 1
        ev = Ev(self.esem[eng], self.ecnt[eng], eng)

        def run(e, waits=waits, fn=fn, ev=ev):
            for s, v in waits:
                e.wait_ge(s, v)
            fn(e).then_inc(ev.sem, 1)

        self.q[eng].append(run)
        self._commit(ev, reads, writes)
        return ev

    def dma(self, eng, key, out, in_, reads=(), writes=()):
        if key not in self.dsem:
            self.dsem[key] = self.stack.enter_context(self.nc.semaphore("ds%d" % len(self.dsem)))
            self.dcnt[key] = 0
        waits = self._waits(eng, reads, writes)
        self.dcnt[key] += 16
        ev = Ev(self.dsem[key], self.dcnt[key], "dma")

        def run(e, waits=waits, ev=ev, out=out, in_=in_):
            for s, v in waits:
                e.wait_ge(s, v)
            e.dma_start(out=out, in_=in_).then_inc(ev.sem, 16)

        self.q[eng].append(run)
        self._commit(ev, reads, writes)
        return ev

    def allgather(self, src, dst, groups, reads=(), writes=()):
        waits = self._waits("pool", reads, writes)
        self.cccnt += 1
        ev = Ev(self.ccsem, self.cccnt, "cc")

        def run(e, waits=waits, ev=ev):
            for s, v in waits:
                e.wait_ge(s, v)
            e.collective_compute("AllGather", ALU.bypass, replica_groups=groups,
                                 ins=[src], outs=[dst]).then_inc(ev.sem)

        self.q["pool"].append(run)
        self._commit(ev, reads, writes)
        return ev

    def barrier(self):
        evs = [(self.esem[e], self.ecnt[e]) for e in self.esem if self.ecnt[e] > 0]
        evs += [(self.dsem[k], self.dcnt[k]) for k in self.dsem if self.dcnt[k] > 0]
        if self.cccnt > 0:
            evs.append((self.ccsem, self.cccnt))
        for eng in self.ENG:
            need = []
            seen = self.seen[eng]
            for s, v in evs:
                if seen.get(id(s), 0) >= v:
                    continue
                seen[id(s)] = v
                need.append((s, v))

            def run(e, need=need):
                for s, v in need:
                    e.wait_ge(s, v)

            self.q[eng].append(run)


def build_program(SEQ, PAST, DEPTH):
    NPT = SEQ // 128
    TT = SEQ + NS * DS
    NQT = SEQ // QW
    NCH = TT // CW
    NPB = PAST // 128
    NTILE = NPT + NS
    assert SEQ % CW == 0 and PAST % 512 == 0

    nc = bass.Bass("TRN2", target_bir_lowering=False)
    dt = nc.dram_tensor
    xin = dt("xin", [TT, D], F32, kind="ExternalInput").ap()
    w_in = dt("w_in_h", [DEPTH, D, D], F32, kind="ExternalInput").ap()
    w_out = dt("w_out", [DEPTH, D, D], F32, kind="ExternalInput").ap()
    normw = dt("normw_b", [DEPTH + 1, 128, D], F32, kind="ExternalInput").ap()
    lam_in = dt("lam_b", [2, 128, 512], F32, kind="ExternalInput").ap()
    subw_in = dt("subw_b", [2, 128, HD], F32, kind="ExternalInput").ap()
    ck = dt("ck", [DEPTH, NS, PAST, HD], F32, kind="ExternalInput").ap()
    cv = dt("cv", [DEPTH, NS, PAST, HD], F32, kind="ExternalInput").ap()
    rope = dt("rope", [TT, 192], F32, kind="ExternalInput").ap()
    cst = dt("cst", [128, 1344], F32, kind="ExternalInput").ap()
    y_out = dt("y", [TT, D], F32, kind="ExternalOutput").ap()
    kout = dt("kout", [DEPTH, TT, HD], F32, kind="ExternalOutput").ap()
    vout = dt("vout", [DEPTH, TT, HD], F32, kind="ExternalOutput").ap()
    xs = dt("xs", [TT, D], F32, kind="Internal").ap()
    qT = dt("qT", [2, 128, TT], BF16, kind="Internal").ap()
    sg = dt("sg", [TT, HD], F32, kind="Internal").ap()
    ag_src = dt("ag_src", [NCH, 256, CW], BF16, kind="Internal").ap()
    ag_dst = dt("ag_dst", [NCH, 1024, CW], BF16, kind="Internal").ap()
    GROUPS = [[0, 1, 2, 3], [4, 5, 6, 7]]

    stack = ExitStack()
    with stack:
        sb = lambda name, shape, dtp: stack.enter_context(nc.sbuf_tensor(name, shape, dtp))
        KT = sb("KT", [128, 2, SEQ], BF16)
        VT = sb("VT", [128, NPT, 257], BF16)
        KTN = sb("KTN", [128, 2, NS * DS], BF16)
        VN = sb("VN", [128, NS, 257], BF16)
        CB = sb("CB", [128, 1344], BF16)
        NEGONES = sb("NEGONES", [128, 128], BF16)
        SW = sb("SW", [128, 2, HD], F32)
        NEGLAM = sb("NEGLAM", [128, 2], F32)
        SMALL = sb("SMALL", [128, 16], F32)
        ZERO1 = sb("ZERO1", [128, 1], F32)
        EPSB = sb("EPSB", [128, 1], F32)
        A16 = sb("A16", [128, 22272], BF16)
        A32 = sb("A32", [128, 5376], F32)
        PS = [stack.enter_context(nc.psum_tensor("ps%d" % i, [128, 512], F32)) for i in range(8)]
        block = stack.enter_context(nc.Block())

        MASKS = CB[:, 0:1024].rearrange("p (r q) -> p r q", r=4)
        MSAMP = CB[:, 1024:1088]
        IDENT = CB[:, 1088:1216]
        TRI = CB[:, 1216:1344]

        class Carver:
            def __init__(self, t):
                self.t, self.o = t, 0

            def take(self, n):
                v = self.t[:, self.o:self.o + n]
                self.o += n
                assert self.o <= self.t.shape[1], (self.o, self.t.shape)
                return v

        c16 = Carver(A16)
        WI = c16.take(8192).rearrange("p (c n) -> p c n", c=8)
        WO = c16.take(8192).rearrange("p (c n) -> p c n", c=8)
        Gs = [c16.take(1024).rearrange("p (c t) -> p c t", c=8) for _ in range(2)]
        Hb = c16.take(1024)
        HTs = [c16.take(1024).rearrange("p (c t) -> p c t", c=8) for _ in range(2)]
        QKB = c16.take(512)
        QTst = c16.take(256).rearrange("p (c t) -> p c t", c=2)
        c32 = Carver(A32)
        Xs = [c32.take(1024) for _ in range(2)]
        NW = c32.take(1024)
        QK = c32.take(512)
        TMPr = c32.take(512)
        VF = c32.take(256)
        SGs = c32.take(256)
        ROPE = [c32.take(192) for _ in range(2)]
        a16 = Carver(A16)
        QTa = [a16.take(512).rearrange("p (c t) -> p c t", c=2) for _ in range(2)]
        Pt = [a16.take(512).rearrange("p (m t) -> p m t", m=2) for _ in range(3)]
        LKt = [a16.take(256) for _ in range(3)]
        OGB = a16.take(256)
        OGT = [a16.take(512).rearrange("p (c t) -> p c t", c=2) for _ in range(2)]
        KPB = a16.take(1024).rearrange("p (j e) -> p j e", j=4)
        KTS2 = [a16.take(2 * PAST).rearrange("p (c t) -> p c t", c=2) for _ in range(2)]
        VS2 = [a16.take(NPB * 257).rearrange("p (j e) -> p j e", j=NPB) for _ in range(2)]
        a32 = Carver(A32)
        SGt = [a32.take(512).rearrange("p (u e) -> p u e", u=2) for _ in range(2)]
        Et = [a32.take(256) for _ in range(2)]
        TMt = [a32.take(256) for _ in range(2)]
        Ot = a32.take(256)
        O2t = a32.take(256)

        P = Prog(nc, stack)

        B = {}

        def buf(name):
            if name not in B:
                B[name] = Buf(name)
            return B[name]

        psb = [buf("ps%d" % i) for i in range(8)]

        P.dma("pool", "cst", CB[:], cst, writes=[buf("CB")])
        P.op("pool", lambda e: e.memset(NEGONES[:], -1.0), writes=[buf("NEGONES")])
        P.op("pool", lambda e: e.memset(VT[:, :, 256:257], 1.0), writes=[buf("VTones")])
        P.op("pool", lambda e: e.memset(VN[:, :, 256:257], 1.0), writes=[buf("VNones")])
        P.op("pool", lambda e: e.memset(ZERO1[:], 0.0), writes=[buf("ZERO1")])
        P.op("pool", lambda e: e.memset(EPSB[:], EPS), writes=[buf("EPSB")])
        LAMT = A32[:, 0:512].rearrange("p (a d) -> p a d", a=4)
        LJ = A32[:, 512:640]
        for j in range(2):
            if 2 * j >= DEPTH:
                break
            lam_init = 0.8 - 0.6 * math.exp(-0.3 * (2 * j))
            P.dma("sp", "lam", LAMT, lam_in[j].rearrange("p (a d) -> p a d", a=4), writes=[buf("LAMT")])
            P.dma("sp", "subw", SW[:, j, :], subw_in[j], writes=[buf("SWraw")])
            for t in range(2):
                P.op("dve", lambda e, t=t: e.tensor_tensor(out=LJ, in0=LAMT[:, 2 * t, :], in1=LAMT[:, 2 * t + 1, :], op=ALU.mult),
                     reads=[buf("LAMT")], writes=[buf("LJ")])
                P.op("dve", lambda e, t=t: e.tensor_reduce(out=SMALL[:, t:t + 1], in_=LJ, axis=AX.X, op=ALU.add),
                     reads=[buf("LJ")], writes=[buf("SMALL")])
            P.op("act", lambda e: e.activation(out=SMALL[:, 2:4], in_=SMALL[:, 0:2], func=AF.Exp),
                 reads=[buf("SMALL")], writes=[buf("SMALL")])
            P.op("dve", lambda e: e.tensor_tensor(out=SMALL[:, 4:5], in0=SMALL[:, 3:4], in1=SMALL[:, 2:3], op=ALU.subtract),
                 reads=[buf("SMALL")], writes=[buf("SMALL")])
            P.op("dve", lambda e, j=j, li=lam_init: e.tensor_scalar(out=NEGLAM[:, j:j + 1], in0=SMALL[:, 4:5], scalar1=-li, scalar2=None, op0=ALU.add),
                 reads=[buf("SMALL")], writes=[buf("NEGLAM")])
            P.op("dve", lambda e, j=j, li=lam_init: e.tensor_scalar(out=SW[:, j, :], in0=SW[:, j, :], scalar1=1.0 - li, scalar2=None, op0=ALU.mult),
                 reads=[buf("SWraw")], writes=[buf("SWraw")])
        P.barrier()

        def tile_info(ti):
            if ti < NPT:
                return ti * 128, 128
            return SEQ + (ti - NPT) * DS, DS

        def f_phase(l):
            last = (l == DEPTH)
            is_diff = (l % 2 == 0)
            if not last:
                P.dma("pool", "WI", WI, w_in[l].rearrange("(c p) n -> p c n", p=128), writes=[buf("WI")])
            if l > 0:
                P.dma("pool", "WO", WO, w_out[l - 1].rearrange("(c p) n -> p c n", p=128), writes=[buf("WO")])
            P.dma("sp", "NW", NW, normw[l], writes=[buf("NW")])
            Y = [PS[0], PS[1]]
            HTp = PS[2][:].bitcast(BF16).rearrange("p (c t) -> p c t", c=8)
            PJs = [[PS[3], PS[4]], [PS[6], PS[7]]]
            pjb = [[psb[3], psb[4]], [psb[6], psb[7]]]
            QKTp = PS[5][:].bitcast(BF16)[:, 0:512].rearrange("p (c t) -> p c t", c=4)

            def front_a(ti):
                t0, n = tile_info(ti)
                sl = ti % 2
                X = Xs[sl]
                Xb = buf("X%d" % sl)
                P.dma("sp", ("X", sl), X[0:n, :], (xin if l == 0 else xs)[t0:t0 + n, :], writes=[Xb])
                if l > 0:
                    G = Gs[sl]
                    Gb = buf("G%d" % sl)
                    ch, off = t0 // CW, t0 % CW
                    P.dma("sp", ("G", sl), G[:, :, 0:n],
                          ag_dst[ch].rearrange("(c p) t -> p c t", p=128)[:, :, off:off + n], writes=[Gb])

                    def mm_out(e, G=G, n=n):
                        ins = None
                        for nn in range(2):
                            for c in range(8):
                                ins = e.matmul(Y[nn][0:n, :], lhsT=G[:, c, 0:n], rhs=WO[:, c, nn * 512:(nn + 1) * 512],
                                               start=(c == 0), stop=(c == 7))
                        return ins
                    P.op("pe", mm_out, reads=[Gb, buf("WO")], writes=[psb[0], psb[1]])
                    for nn in range(2):
                        P.op("dve", lambda e, X=X, n=n, nn=nn: e.tensor_tensor(
                            out=X[0:n, nn * 512:(nn + 1) * 512], in0=X[0:n, nn * 512:(nn + 1) * 512],
                            in1=Y[nn][0:n, :], op=ALU.add), reads=[psb[nn], Xb], writes=[Xb])
                if not last:
                    P.dma("pool", ("Xst", sl), xs[t0:t0 + n, :], X[0:n, :], reads=[Xb])
                SSb = buf("SS")
                P.op("act", lambda e, X=X, n=n: e.activation(out=Hb[0:n, :], in_=X[0:n, :], func=AF.Square,
                                                              accum_out=SMALL[0:n, 8:9]),
                     reads=[Xb], writes=[buf("H"), SSb])
                P.op("act", lambda e, n=n: e.activation(out=SMALL[0:n, 9:10], in_=SMALL[0:n, 8:9], func=AF.Ln,
                                                        scale=1.0 / D, bias=EPSB[0:n, 0:1]),
                     reads=[SSb], writes=[buf("RS0")])
                P.op("act", lambda e, n=n: e.activation(out=SMALL[0:n, 10:11], in_=SMALL[0:n, 9:10], func=AF.Exp, scale=-0.5),
                     reads=[buf("RS0")], writes=[buf("RS")])
                if last:
                    P.op("dve", lambda e, X=X, n=n: e.scalar_tensor_tensor(
                        out=X[0:n, :], in0=X[0:n, :], scalar=SMALL[0:n, 10:11], in1=NW[0:n, :], op0=ALU.mult, op1=ALU.mult),
                        reads=[Xb, buf("RS"), buf("NW")], writes=[Xb])
                    P.dma("pool", ("Yst", sl), y_out[t0:t0 + n, :], X[0:n, :], reads=[Xb])
                    return
                P.op("dve", lambda e, X=X, n=n: e.scalar_tensor_tensor(
                    out=Hb[0:n, :], in0=X[0:n, :], scalar=SMALL[0:n, 10:11], in1=NW[0:n, :], op0=ALU.mult, op1=ALU.mult),
                    reads=[Xb, buf("RS"), buf("NW")], writes=[buf("H")])

            def front_b(ti):
                t0, n = tile_info(ti)
                sl = ti % 2
                HT = HTs[sl]
                HTb = buf("HT%d" % sl)

                def tr_h(e, n=n):
                    ins = None
                    for c in range(8):
                        ins = e.transpose(out=HTp[:, c, 0:n], in_=Hb[0:n, c * 128:(c + 1) * 128], identity=IDENT[0:n, 0:n])
                    return ins
                P.op("pe", tr_h, reads=[buf("H"), buf("CB")], writes=[psb[2]])
                P.op("act", lambda e, n=n: e.activation(out=HT[:, :, 0:n], in_=HTp[:, :, 0:n], func=AF.Copy),
                     reads=[psb[2]], writes=[HTb])

            def back_a(ti):
                t0, n = tile_info(ti)
                sl = ti % 2
                HT = HTs[sl]
                HTb = buf("HT%d" % sl)
                PJ = PJs[sl]
                pb3, pb4 = pjb[sl]

                def mm_in(e, n=n):
                    ins = None
                    for nn in range(2):
                        for c in range(8):
                            ins = e.matmul(PJ[nn][0:n, :], lhsT=HT[:, c, 0:n], rhs=WI[:, c, nn * 512:(nn + 1) * 512],
                                           start=(c == 0), stop=(c == 7))
                    return ins
                P.op("pe", mm_in, reads=[HTb, buf("WI")], writes=[pb3, pb4])
                QKb = buf("QK")
                if is_diff:
                    R = ROPE[sl]
                    Rb = buf("ROPE%d" % sl)
                    P.dma("sp", ("ROPE", sl), R[0:n, :], rope[t0:t0 + n, :], writes=[Rb])
                    pj8 = PJ[0][0:n, :].rearrange("p (m d) -> p m d", m=8)
                    pj4 = PJ[0][0:n, :].rearrange("p (m h d) -> p m h d", m=4, h=2)
                    tm4 = TMPr[0:n, :].rearrange("p (m h d) -> p m h d", m=4, h=2)
                    P.op("dve", lambda e, n=n, R=R, pj8=pj8: e.tensor_tensor(
                        out=QK[0:n, :].rearrange("p (m d) -> p m d", m=8), in0=pj8,
                        in1=R[0:n, 0:64].unsqueeze(1).broadcast_to([n, 8, 64]), op=ALU.mult),
                        reads=[pb3, Rb], writes=[QKb])
                    P.op("dve", lambda e, n=n, R=R, pj4=pj4, tm4=tm4: e.tensor_tensor(
                        out=tm4[:, :, 0, :], in0=pj4[:, :, 1, :],
                        in1=R[0:n, 64:128].unsqueeze(1).broadcast_to([n, 4, 64]), op=ALU.mult),
                        reads=[pb3, Rb], writes=[buf("TMPa")])
                    P.op("dve", lambda e, n=n, R=R, pj4=pj4, tm4=tm4: e.tensor_tensor(
                        out=tm4[:, :, 1, :], in0=pj4[:, :, 0, :],
                        in1=R[0:n, 128:192].unsqueeze(1).broadcast_to([n, 4, 64]), op=ALU.mult),
                        reads=[pb3, Rb], writes=[buf("TMPb")])
                    P.op("dve", lambda e, n=n: e.tensor_tensor(out=QK[0:n, :], in0=QK[0:n, :], in1=TMPr[0:n, :], op=ALU.add),
                         reads=[QKb, buf("TMPa"), buf("TMPb")], writes=[QKb])
                else:
                    P.op("dve", lambda e, n=n: e.tensor_copy(out=QK[0:n, :], in_=PJ[0][0:n, :]),
                         reads=[pb3], writes=[QKb])
                P.dma("pool", "kst", kout[l, t0:t0 + n, :], QK[0:n, 256:512], reads=[QKb])
                qs = 1.0 if is_diff else SB_SCALE
                P.op("act", lambda e, n=n, qs=qs: e.activation(out=QKB[0:n, 0:256], in_=QK[0:n, 0:256], func=AF.Copy, scale=qs),
                     reads=[QKb], writes=[buf("QKBq")])
                P.op("act", lambda e, n=n: e.activation(out=QKB[0:n, 256:512], in_=QK[0:n, 256:512], func=AF.Copy),
                     reads=[QKb], writes=[buf("QKBk")])
                if ti < NPT:
                    vdst = VT[0:n, ti, 0:256]
                else:
                    vdst = VN[0:n, ti - NPT, 0:256]
                P.op("dve", lambda e, n=n: e.tensor_copy(out=VF[0:n, :], in_=PJ[1][0:n, 0:256]),
                     reads=[pb4], writes=[buf("VF")])
                P.op("dve", lambda e, n=n, vdst=vdst: e.tensor_copy(out=vdst, in_=PJ[1][0:n, 0:256]),
                     reads=[pb4], writes=[])
                P.dma("pool", "vst", vout[l, t0:t0 + n, :], VF[0:n, :], reads=[buf("VF")])
                P.op("act", lambda e, n=n: e.activation(out=SGs[0:n, :], in_=PJ[1][0:n, 256:512], func=AF.Silu),
                     reads=[pb4], writes=[buf("SGs")])
                P.dma("pool", "gst", sg[t0:t0 + n, :], SGs[0:n, :], reads=[buf("SGs")])

            def back_b(ti):
                t0, n = tile_info(ti)

                def tr_qk(e, n=n):
                    ins = None
                    for c in range(4):
                        ins = e.transpose(out=QKTp[:, c, 0:n], in_=QKB[0:n, c * 128:(c + 1) * 128], identity=IDENT[0:n, 0:n])
                    return ins
                P.op("pe", tr_qk, reads=[buf("QKBq"), buf("QKBk")], writes=[psb[5]])
                P.op("act", lambda e, n=n: e.activation(out=QTst[:, :, 0:n], in_=QKTp[:, 0:2, 0:n], func=AF.Copy),
                     reads=[psb[5]], writes=[buf("QTst")])
                if ti < NPT:
                    kdst = KT[:, :, t0:t0 + n]
                else:
                    s_ = ti - NPT
                    kdst = KTN[:, :, s_ * DS:(s_ + 1) * DS]
                P.op("act", lambda e, n=n, kdst=kdst: e.activation(out=kdst, in_=QKTp[:, 2:4, 0:n], func=AF.Copy),
                     reads=[psb[5]], writes=[])
                P.dma("pool", "qst", qT[:, :, t0:t0 + n].rearrange("c p t -> p c t"), QTst[:, :, 0:n], reads=[buf("QTst")])

            front_a(0)
            if not last:
                front_b(0)
            for ti in range(NTILE):
                if ti + 1 < NTILE:
                    front_a(ti + 1)
                if not last:
                    back_a(ti)
                    if ti + 1 < NTILE:
                        front_b(ti + 1)
                    back_b(ti)
            P.barrier()

        state = {"step": 0, "qt": 0}

        def a_phase(l):
            is_diff = (l % 2 == 0)
            mbase = 0 if is_diff else 2
            jl = l // 2
            tiles = [("p", qt) for qt in range(NQT)] + [("s", s) for s in range(NS)]
            NTL = len(tiles)
            TP = PS[7][:].bitcast(BF16)[:, 0:512].rearrange("p (c t) -> p c t", c=2)
            TPk = PS[7][:].bitcast(BF16).rearrange("p (j c t) -> p j c t", j=4, c=2)
            if is_diff:
                SBK = [0, 1]
                ACC = [[PS[2 + 2 * m + u] for u in range(2)] for m in range(2)]
                accb = [psb[2], psb[3], psb[4], psb[5]]
            else:
                SBK = [0, 1, 3]
                ACC = [PS[2][:, 0:256], PS[2][:, 256:512]]
                accb = [psb[2]]
                CACC = PS[6]

            def describe(ti):
                kind, idx = tiles[ti]
                sl = ti % 2
                QTb, SGb = buf("QTa%d" % sl), buf("SGt%d" % sl)
                if kind == "p":
                    qt = idx
                    t0 = qt * QW
                    kds = []
                    for kb in range(2 * qt + 2):
                        rdiag = kb - 2 * qt
                        kds.append(dict(K=[KT[:, c, kb * 128:(kb + 1) * 128] for c in range(2)], V=VT[:, kb, :], nk=128,
                                        mask=(MASKS[:, mbase + rdiag, :] if rdiag >= 0 else None), reads=[]))
                    qd = dict(QT=QTa[sl], QTb=QTb, SG=SGt[sl], SGb=SGb, qw=QW, nsub=2, subw=128,
                              out=(t0 // CW, t0 % CW), ag=((t0 + QW) % CW == 0))
                else:
                    s = idx
                    KTSb, VSb = buf("KTS%d" % sl), buf("VS%d" % sl)
                    kds = []
                    for jb in range(NPB):
                        kds.append(dict(K=[KTS2[sl][:, c, jb * 128:(jb + 1) * 128] for c in range(2)], V=VS2[sl][:, jb, :],
                                        nk=128, mask=None, reads=[KTSb, VSb]))
                    kds.append(dict(K=[KTN[:, c, s * DS:(s + 1) * DS] for c in range(2)], V=VN[:, s, :], nk=DS,
                                    mask=(None if is_diff else MSAMP[0:DS, :]), reads=[]))
                    qd = dict(QT=QTa[sl], QTb=QTb, SG=SGt[sl], SGb=SGb, qw=DS, nsub=1, subw=DS,
                              out=(NCH - 1, s * DS), ag=(s == NS - 1))
                if not is_diff:
                    kds = kds[::-1]
                return qd, kds

            desc = [describe(ti) for ti in range(NTL)]

            def emit_prep(ti):
                kind, idx = tiles[ti]
                sl = ti % 2
                QTb, SGb = buf("QTa%d" % sl), buf("SGt%d" % sl)
                if kind == "p":
                    t0 = idx * QW
                    P.dma("sp", ("QTa", sl), QTa[sl][:, :, 0:QW], qT[:, :, t0:t0 + QW].rearrange("c p t -> p c t"), writes=[QTb])
                    P.dma("sp", ("SGt", sl), SGt[sl][:, :, :], sg[t0:t0 + QW, :].rearrange("(u p) e -> p u e", p=128), writes=[SGb])
                    return
                s = idx
                t0 = SEQ + s * DS
                KTSs, VSs = KTS2[sl], VS2[sl]
                KTSb, VSb = buf("KTS%d" % sl), buf("VS%d" % sl)
                P.dma("sp", ("QTa", sl), QTa[sl][:, :, 0:DS], qT[:, :, t0:t0 + DS].rearrange("c p t -> p c t"), writes=[QTb])
                P.dma("sp", ("SGt", sl), SGt[sl][0:DS, 0, :], sg[t0:t0 + DS, :], writes=[SGb])
                P.op("pool", lambda e, VSs=VSs: e.memset(VSs[:, :, 256:257], 1.0), writes=[VSb])
                P.dma("pool", ("VS", sl), VSs[:, :, 0:256], cv[l, s].rearrange("(j p) e -> p j e", p=128), writes=[VSb])
                for g4 in range(NPB // 4):
                    P.dma("pool", "KPB", KPB[:, :, :], ck[l, s, g4 * 512:(g4 + 1) * 512, :].rearrange("(j p) e -> p j e", p=128),
                          writes=[buf("KPB")])

                    def tr_k(e):
                        ins = None
                        for jj in range(4):
                            for c in range(2):
                                ins = e.transpose(out=TPk[:, jj, c, :], in_=KPB[:, jj, c * 128:(c + 1) * 128], identity=IDENT)
                        return ins
                    P.op("pe", tr_k, reads=[buf("KPB"), buf("CB")], writes=[psb[7]])
                    P.op("act", lambda e, g4=g4, KTSs=KTSs: e.activation(
                        out=KTSs[:, :, g4 * 512:(g4 + 1) * 512].rearrange("p c (j t) -> p j c t", j=4), in_=TPk, func=AF.Copy),
                        reads=[psb[7]], writes=[KTSb])

            flat = [(ti, i) for ti in range(NTL) for i in range(len(desc[ti][1]))]
            base = state["step"]
            state["step"] += len(flat)

            def ctx(fi):
                ti, i = flat[fi]
                qd, kds = desc[ti]
                return ti, i, qd, kds[i], len(kds), base + fi

            def d_stage0(fi):
                ti, i, qd, kd, nkt, st = ctx(fi)
                qw, nk, QT = qd["qw"], kd["nk"], qd["QT"]
                Sb = psb[SBK[st % 2]]
                S = PS[SBK[st % 2]][:].rearrange("p (m t) -> p m t", m=2)
                Pv, Pb = Pt[st % 3], buf("P%d" % (st % 3))

                def mm_qk(e):
                    ins = None
                    for m in range(2):
                        ins = e.matmul(S[0:nk, m, 0:qw], lhsT=kd["K"][m], rhs=QT[:, m, 0:qw], start=True, stop=True)
                    return ins
                P.op("pe", mm_qk, reads=[qd["QTb"]] + kd["reads"], writes=[Sb])
                P.op("act", lambda e: e.activation(out=Pv[0:nk, :, 0:qw], in_=S[0:nk, :, 0:qw], func=AF.Exp, scale=DIFF_SCALE),
                     reads=[Sb], writes=[Pb])
                if kd["mask"] is not None:
                    mk = kd["mask"]
                    P.op("dve", lambda e: e.tensor_tensor(out=Pv[0:nk, :, 0:qw], in0=Pv[0:nk, :, 0:qw],
                                                          in1=mk.unsqueeze(1).broadcast_to([nk, 2, qw]), op=ALU.mult),
                         reads=[Pb, buf("CB")], writes=[Pb])

            def d_stage1(fi):
                ti, i, qd, kd, nkt, st = ctx(fi)
                qw, nk, nsub, sw_ = qd["qw"], kd["nk"], qd["nsub"], qd["subw"]
                Pv, Pb = Pt[st % 3], buf("P%d" % (st % 3))

                def mm_pv(e):
                    ins = None
                    for m in range(2):
                        for u in range(nsub):
                            ins = e.matmul(ACC[m][u][0:sw_, 0:257], lhsT=Pv[0:nk, m, u * sw_:(u + 1) * sw_],
                                           rhs=kd["V"][0:nk, 0:257], start=(i == 0), stop=(i == nkt - 1))
                    return ins
                P.op("pe", mm_pv, reads=[Pb] + kd["reads"], writes=accb)
                if i == nkt - 1:
                    epilogue(ti)

            def s_stage0(fi):
                ti, i, qd, kd, nkt, st = ctx(fi)
                qw, nk, QT = qd["qw"], kd["nk"], qd["QT"]
                Sb, Z = psb[SBK[st % 3]], PS[SBK[st % 3]]
                E, Eb = Et[st % 2], buf("E%d" % (st % 2))
                LK, LKb = LKt[st % 3], buf("LK%d" % (st % 3))

                def mm_z(e):
                    ins = None
                    for c in range(2):
                        ins = e.matmul(Z[0:nk, 0:qw], lhsT=kd["K"][c], rhs=QT[:, c, 0:qw], start=(c == 0), stop=False)
                    return ins
                P.op("pe", mm_z, reads=[qd["QTb"]] + kd["reads"], writes=[Sb])
                P.op("act", lambda e: e.activation(out=E[0:nk, 0:qw], in_=Z[0:nk, 0:qw], func=AF.Exp), reads=[Sb], writes=[Eb])
                P.op("act", lambda e: e.activation(out=LK[0:nk, 0:qw], in_=E[0:nk, 0:qw], func=AF.Ln, bias=1.0),
                     reads=[Eb], writes=[LKb])
                if kd["mask"] is not None:
                    mk = kd["mask"]
                    P.op("dve", lambda e: e.tensor_tensor(out=LK[0:nk, 0:qw], in0=LK[0:nk, 0:qw], in1=mk, op=ALU.mult),
                         reads=[LKb, buf("CB")], writes=[LKb])

            def s_stage1(fi):
                ti, i, qd, kd, nkt, st = ctx(fi)
                qw, nk = qd["qw"], kd["nk"]
                Sb, Z = psb[SBK[st % 3]], PS[SBK[st % 3]]
                LK, LKb = LKt[st % 3], buf("LK%d" % (st % 3))
                TM, TMb = TMt[st % 2], buf("TM%d" % (st % 2))
                Pv2, Pb = Pt[st % 3][:, 0, :], buf("P%d" % (st % 3))

                def mm_tri(e):
                    e.matmul(Z[0:nk, 0:qw], lhsT=TRI[0:nk, 0:nk], rhs=LK[0:nk, 0:qw], start=False, stop=True)
                    return e.matmul(CACC[:, 0:qw], lhsT=NEGONES[0:nk, :], rhs=LK[0:nk, 0:qw],
                                    start=(i == 0), stop=(i == nkt - 1))
                P.op("pe", mm_tri, reads=[LKb, buf("CB"), buf("NEGONES")], writes=[Sb, psb[6]])
                P.op("dve", lambda e: e.tensor_copy(out=TM[0:nk, 0:qw], in_=CACC[0:nk, 0:qw]), reads=[psb[6]], writes=[TMb])
                P.op("dve", lambda e: e.tensor_tensor(out=TM[0:nk, 0:qw], in0=TM[0:nk, 0:qw], in1=Z[0:nk, 0:qw], op=ALU.add),
                     reads=[Sb, TMb], writes=[TMb])

            def s_stage1b(fi):
                ti, i, qd, kd, nkt, st = ctx(fi)
                qw, nk = qd["qw"], kd["nk"]
                TM, TMb = TMt[st % 2], buf("TM%d" % (st % 2))
                Pv2, Pb = Pt[st % 3][:, 0, :], buf("P%d" % (st % 3))
                P.op("act", lambda e: e.activation(out=Pv2[0:nk, 0:qw], in_=TM[0:nk, 0:qw], func=AF.Exp), reads=[TMb], writes=[Pb])
                if kd["mask"] is not None:
                    mk = kd["mask"]
                    P.op("dve", lambda e: e.tensor_tensor(out=Pv2[0:nk, 0:qw], in0=Pv2[0:nk, 0:qw], in1=mk, op=ALU.mult),
                         reads=[Pb, buf("CB")], writes=[Pb])

            def s_stage2(fi):
                ti, i, qd, kd, nkt, st = ctx(fi)
                qw, nk, nsub, sw_ = qd["qw"], kd["nk"], qd["nsub"], qd["subw"]
                Pv2, Pb = Pt[st % 3][:, 0, :], buf("P%d" % (st % 3))

                def mm_pv(e):
                    ins = None
                    for u in range(nsub):
                        ins = e.matmul(ACC[u][0:sw_, 0:256], lhsT=Pv2[0:nk, u * sw_:(u + 1) * sw_],
                                       rhs=kd["V"][0:nk, 0:256], start=(i == 0), stop=(i == nkt - 1))
                    return ins
                P.op("pe", mm_pv, reads=[Pb] + kd["reads"], writes=accb)
                if i == nkt - 1:
                    epilogue(ti)

            def epilogue(ti):
                qd, kds = desc[ti]
                qw, nsub, sw_ = qd["qw"], qd["nsub"], qd["subw"]
                SGv, SGb = qd["SG"], qd["SGb"]
                qi = state["qt"]
                state["qt"] += 1
                OG = OGT[qi % 2]
                OGb_ = buf("OGT%d" % (qi % 2))
                r = slice(0, sw_)
                for u in range(nsub):
                    if is_diff:
                        for m in range(2):
                            P.op("dve", lambda e, m=m, u=u: e.reciprocal(out=SMALL[r, m:m + 1], in_=ACC[m][u][r, 256:257]),
                                 reads=accb, writes=[buf("R%d" % m)])
                        P.op("dve", lambda e: e.tensor_tensor(out=SMALL[r, 2:3], in0=SMALL[r, 1:2], in1=NEGLAM[r, jl:jl + 1], op=ALU.mult),
                             reads=[buf("R1")], writes=[buf("R1l")])
                        P.op("dve", lambda e, u=u: e.tensor_scalar(out=Ot[r, :], in0=ACC[0][u][r, 0:256], scalar1=SMALL[r, 0:1],
                                                                    scalar2=None, op0=ALU.mult),
                             reads=accb + [buf("R0")], writes=[buf("O")])
                        P.op("dve", lambda e, u=u: e.scalar_tensor_tensor(out=Ot[r, :], in0=ACC[1][u][r, 0:256], scalar=SMALL[r, 2:3],
                                                                           in1=Ot[r, :], op0=ALU.mult, op1=ALU.add),
                             reads=accb + [buf("R1l"), buf("O")], writes=[buf("O")])
                        P.op("dve", lambda e: e.tensor_tensor(out=O2t[r, :], in0=Ot[r, :], in1=Ot[r, :], op=ALU.mult),
                             reads=[buf("O")], writes=[buf("O2")])
                        P.op("dve", lambda e: e.tensor_reduce(out=SMALL[r, 3:4], in_=O2t[r, :], axis=AX.X, op=ALU.add),
                             reads=[buf("O2")], writes=[buf("SS2")])
                        P.op("act", lambda e: e.activation(out=SMALL[r, 4:5], in_=SMALL[r, 3:4], func=AF.Ln, scale=1.0 / HD,
                                                           bias=EPSB[r, 0:1]),
                             reads=[buf("SS2")], writes=[buf("RS2a")])
                        P.op("act", lambda e: e.activation(out=SMALL[r, 5:6], in_=SMALL[r, 4:5], func=AF.Exp, scale=-0.5),
                             reads=[buf("RS2a")], writes=[buf("RS2")])
                        P.op("dve", lambda e: e.scalar_tensor_tensor(out=Ot[r, :], in0=Ot[r, :], scalar=SMALL[r, 5:6], in1=SW[r, jl, :],
                                                                     op0=ALU.mult, op1=ALU.mult),
                             reads=[buf("O"), buf("RS2")], writes=[buf("O")])
                        P.op("dve", lambda e, u=u: e.tensor_tensor(out=OGB[r, :], in0=Ot[r, :], in1=SGv[r, u, :], op=ALU.mult),
                             reads=[buf("O"), SGb], writes=[buf("OGB")])
                    else:
                        P.op("dve", lambda e, u=u: e.tensor_tensor(out=OGB[r, :], in0=ACC[u][r, :], in1=SGv[r, u, :], op=ALU.mult),
                             reads=accb + [SGb], writes=[buf("OGB")])

                    def tr_o(e, u=u):
                        ins = None
                        for c in range(2):
                            ins = e.transpose(out=TP[:, c, u * sw_:(u + 1) * sw_], in_=OGB[r, c * 128:(c + 1) * 128],
                                              identity=IDENT[r, r])
                        return ins
                    P.op("pe", tr_o, reads=[buf("OGB"), buf("CB")], writes=[psb[7]])
                P.op("dve", lambda e: e.tensor_copy(out=OG[:, :, 0:qw], in_=TP[:, :, 0:qw]), reads=[psb[7]], writes=[OGb_])
                ch, off = qd["out"]
                P.dma("pool", ("ogst", qi % 2), ag_src[ch].rearrange("(c p) t -> p c t", p=128)[:, :, off:off + qw],
                      OG[:, :, 0:qw], reads=[OGb_], writes=[buf("agsrc%d" % ch)])
                if qd["ag"]:
                    P.allgather(ag_src[ch], ag_dst[ch], GROUPS, reads=[buf("agsrc%d" % ch)], writes=[buf("agdst%d" % ch)])
                if ti + 2 < NTL:
                    emit_prep(ti + 2)

            stages = [d_stage0, d_stage1] if is_diff else [s_stage0, s_stage1, s_stage1b, s_stage2]
            NSTG = len(stages)
            emit_prep(0)
            emit_prep(1)
            for k in range(-(NSTG - 1), len(flat)):
                for sg_i in range(NSTG):
                    fi = k + (NSTG - 1 - sg_i)
                    if 0 <= fi < len(flat):
                        stages[sg_i](fi)
            P.barrier()

        for l in range(DEPTH):
            f_phase(l)
            a_phase(l)
        f_phase(DEPTH)

        @block.tensor
        def _(e):
            for f in P.q["pe"]:
                f(e)

        @block.scalar
        def _(e):
            for f in P.q["act"]:
                f(e)

        @block.vector
        def _(e):
            for f in P.q["dve"]:
                f(e)

        @block.gpsimd
        def _(e):
            for f in P.q["pool"]:
                f(e)

        @block.sync
        def _(e):
            for f in P.q["sp"]:
                f(e)
    return nc


_CACHE = {}


def _consts(SEQ, PAST):
    TT = SEQ + NS * DS
    half = 64
    inv = (10000.0 ** (-np.arange(half, dtype=np.float32) / half)).astype(np.float32)
    pos = np.concatenate([np.arange(SEQ), np.tile(PAST + np.arange(DS), NS)]).astype(np.float32)
    ang = pos[:, None] * inv[None, :]
    cos, sin = np.cos(ang).astype(np.float32), np.sin(ang).astype(np.float32)
    rope = np.concatenate([cos, -sin, sin], axis=1).astype(np.float32)
    k = np.arange(128)[:, None]
    q = np.arange(QW)[None, :]
    cst = np.zeros((128, 1344), np.float32)
    for r in range(2):
        cst[:, r * 256:(r + 1) * 256] = ((128 * r + k) // 64 <= q // 64)
        cst[:, 512 + r * 256:512 + (r + 1) * 256] = ((128 * r + k) < q)
    cst[:, 1024:1088] = (k < np.arange(64)[None, :])
    cst[:, 1088:1216] = np.eye(128)
    cst[:, 1216:1344] = (k < np.arange(128)[None, :])
    return rope, cst


def kernel(x_prompt, x_sample, cache_k, cache_v, norm_w, w_in, w_out, diff_lambda, diff_subln_w, final_norm_w):
    f = lambda a: np.ascontiguousarray(np.asarray(a, dtype=np.float32))
    x_prompt, x_sample, cache_k, cache_v = f(x_prompt), f(x_sample), f(cache_k), f(cache_v)
    norm_w, w_in, w_out = f(norm_w), f(w_in), f(w_out)
    diff_lambda, diff_subln_w, final_norm_w = f(diff_lambda), f(diff_subln_w), f(final_norm_w)
    BATCH, SEQ, _ = x_prompt.shape
    DEPTH, DEC_BATCH, PAST, _ = cache_k.shape
    assert BATCH == 2 and DEC_BATCH == 2 * NS and x_sample.shape[1] == DS
    TT = SEQ + NS * DS
    key = (SEQ, PAST, DEPTH)
    if key not in _CACHE:
        _CACHE[key] = build_program(SEQ, PAST, DEPTH)
    nc = _CACHE[key]
    rope, cst = _consts(SEQ, PAST)
    normw_b = np.ascontiguousarray(np.broadcast_to(
        np.concatenate([norm_w, final_norm_w[None, :]], 0)[:, None, :], (DEPTH + 1, 128, D)))
    nl = diff_lambda.shape[0]
    lam_b = np.zeros((2, 128, 512), np.float32)
    subw_b = np.zeros((2, 128, HD), np.float32)
    lam_b[:nl] = np.broadcast_to(diff_lambda.reshape(nl, 1, 512), (nl, 128, 512))
    subw_b[:nl] = np.broadcast_to(diff_subln_w.reshape(nl, 1, HD), (nl, 128, HD))
    in_maps = []
    for c in range(8):
        b, h = c // 4, c % 4
        cols = np.concatenate([np.arange(j * D + h * HD, j * D + (h + 1) * HD) for j in range(4)])
        in_maps.append({
            "xin": np.ascontiguousarray(np.concatenate([x_prompt[b], x_sample[NS * b:NS * (b + 1)].reshape(NS * DS, D)], 0)),
            "w_in_h": np.ascontiguousarray(w_in[:, :, cols]),
            "w_out": w_out,
            "normw_b": normw_b,
            "lam_b": lam_b,
            "subw_b": subw_b,
            "ck": np.ascontiguousarray(cache_k[:, NS * b:NS * (b + 1), :, h * HD:(h + 1) * HD]),
            "cv": np.ascontiguousarray(cache_v[:, NS * b:NS * (b + 1), :, h * HD:(h + 1) * HD]),
            "rope": rope,
            "cst": cst,
        })
    res = run_bass_kernel_spmd(nc, in_maps, core_ids=list(range(8)))
    y_prompt = np.empty((BATCH, SEQ, D), np.float32)
    y_sample = np.empty((DEC_BATCH, DS, D), np.float32)
    nkp = np.empty((DEPTH, BATCH, SEQ, D), np.float32)
    nvp = np.empty((DEPTH, BATCH, SEQ, D), np.float32)
    nks = np.empty((DEPTH, DEC_BATCH, DS, D), np.float32)
    nvs = np.empty((DEPTH, DEC_BATCH, DS, D), np.float32)
    for c in range(8):
        b, h = c // 4, c % 4
        r = res.results[c]
        if h == 0:
            y_prompt[b] = r["y"][:SEQ]
            y_sample[NS * b:NS * (b + 1)] = r["y"][SEQ:].reshape(NS, DS, D)
        ko, vo = r["kout"], r["vout"]
        nkp[:, b, :, h * HD:(h + 1) * HD] = ko[:, :SEQ]
        nvp[:, b, :, h * HD:(h + 1) * HD] = vo[:, :SEQ]
        nks[:, NS * b:NS * (b + 1), :, h * HD:(h + 1) * HD] = ko[:, SEQ:].reshape(DEPTH, NS, DS, HD)
        nvs[:, NS * b:NS * (b + 1), :, h * HD:(h + 1) * HD] = vo[:, SEQ:].reshape(DEPTH, NS, DS, HD)
    return (y_prompt, y_sample, nkp, nvp, nks, nvs)
```
